# Optimizing a Trainium2 kernel written in Bass

```python
import math
import jax, jax.numpy as jnp
from jax import lax
import numpy as np

D_MODEL = 1024
BATCH = 4
SEQ = 8192
DEPTH = 1

N_META = 16
D_ATTN = D_MODEL // 2
D_CONV = D_MODEL // 2
D_MIX = D_ATTN + D_CONV
HEAD_DIM = 64
N_DIFF_HEADS = D_ATTN // (2 * HEAD_DIM)
CONV_WIDTH = 31
D_FF = 2816
ROPE_THETA = 10000.0
Q_BLOCK = 128
NORM_EPS = 1e-5
D_IN_PROJ = 3 * D_ATTN + 2 * D_CONV

kernel_name = "hymba_diffattn_conformer_macaron"


def rmsnorm(x, g):
    xf = x.astype(jnp.float32)
    y = xf * lax.rsqrt(jnp.mean(xf * xf, axis=-1, keepdims=True) + NORM_EPS)
    return (y * g.astype(jnp.float32)).astype(x.dtype)


def swiglu_ffn(x, w_gate, w_up, w_down):
    return (jax.nn.silu(x @ w_gate) * (x @ w_up)) @ w_down


def rope_tables(length):
    pos = jnp.arange(length, dtype=jnp.float32)
    inv_freq = ROPE_THETA ** (-jnp.arange(0, HEAD_DIM, 2, dtype=jnp.float32) / HEAD_DIM)
    ang = pos[:, None] * inv_freq[None, :]
    return jnp.cos(ang), jnp.sin(ang)


def apply_rope(x, cos, sin):
    half = HEAD_DIM // 2
    x1, x2 = x[..., :half], x[..., half:]
    return jnp.concatenate([x1 * cos - x2 * sin, x2 * cos + x1 * sin], axis=-1)


def diff_attention(q, k, v, lam, subln_w, lambda_init):
    B, L, _ = q.shape
    H = N_DIFF_HEADS
    n_blk = -(-L // Q_BLOCK)
    Lp = n_blk * Q_BLOCK
    pad = ((0, 0), (0, Lp - L), (0, 0))
    q = jnp.pad(q, pad).reshape(B, Lp, 2 * H, HEAD_DIM).transpose(0, 2, 1, 3).astype(jnp.float32)
    k = jnp.pad(k, pad).reshape(B, Lp, 2 * H, HEAD_DIM).transpose(0, 2, 1, 3).astype(jnp.float32)
    v = jnp.pad(v, pad).reshape(B, Lp, H, 2 * HEAD_DIM).transpose(0, 2, 1, 3).astype(jnp.float32)
    cos, sin = rope_tables(Lp)
    q = apply_rope(q, cos, sin) * (HEAD_DIM ** -0.5)
    k = apply_rope(k, cos, sin)
    kpos = jnp.arange(Lp)

    def block(i):
        start = i * Q_BLOCK
        qb = lax.dynamic_slice_in_dim(q, start, Q_BLOCK, axis=2)
        s = jnp.einsum('bhqd,bhkd->bhqk', qb, k)
        qpos = start + jnp.arange(Q_BLOCK)
        s = jnp.where(kpos[None, :] <= qpos[:, None], s, -jnp.inf)
        p = jax.nn.softmax(s, axis=-1).reshape(B, H, 2, Q_BLOCK, Lp)
        p = p[:, :, 0] - lam * p[:, :, 1]
        return jnp.einsum('bhqk,bhkv->bhqv', p, v)

    o = lax.map(block, jnp.arange(n_blk))
    o = o.transpose(1, 0, 3, 2, 4).reshape(B, Lp, H, 2 * HEAD_DIM)[:, :L]
    o = rmsnorm(o, subln_w) * (1.0 - lambda_init)
    return o.reshape(B, L, D_ATTN)


def conformer_conv(u, conv_w, conv_b, ln_g, ln_b):
    a, g = jnp.split(u, 2, axis=-1)
    z = a * jax.nn.sigmoid(g)
    z = lax.conv_general_dilated(
        z, conv_w[:, None, :].astype(z.dtype), window_strides=(1,),
        padding=[(CONV_WIDTH - 1, 0)],
        dimension_numbers=('NWC', 'WIO', 'NWC'),
        feature_group_count=D_CONV) + conv_b
    zf = z.astype(jnp.float32)
    mu = jnp.mean(zf, axis=-1, keepdims=True)
    var = jnp.mean(jnp.square(zf - mu), axis=-1, keepdims=True)
    zf = (zf - mu) * lax.rsqrt(var + NORM_EPS) * ln_g.astype(jnp.float32) + ln_b.astype(jnp.float32)
    return jax.nn.silu(zf).astype(u.dtype)


def setup_inputs(seed: int = 0) -> dict:
    key = jax.random.key(seed)
    ks = jax.random.split(key, 24)
    f32 = jnp.float32

    def nrm(k, shape, scale):
        return jax.random.normal(k, shape, f32) * scale

    def gain(k, shape):
        return 1.0 + 0.02 * jax.random.normal(k, shape, f32)

    return {
        "x": jax.random.normal(ks[0], (BATCH, SEQ, D_MODEL), f32),
        "meta_tokens": nrm(ks[1], (N_META, D_MODEL), 1.0),
        "ffn1_norm": gain(ks[2], (DEPTH, D_MODEL)),
        "ffn1_w_gate": nrm(ks[3], (DEPTH, D_MODEL, D_FF), D_MODEL ** -0.5),
        "ffn1_w_up": nrm(ks[4], (DEPTH, D_MODEL, D_FF), D_MODEL ** -0.5),
        "ffn1_w_down": nrm(ks[5], (DEPTH, D_FF, D_MODEL), D_FF ** -0.5),
        "mix_norm": gain(ks[6], (DEPTH, D_MODEL)),
        "w_in": nrm(ks[7], (DEPTH, D_MODEL, D_IN_PROJ), D_MODEL ** -0.5),
        "lambda_q1": nrm(ks[8], (DEPTH, HEAD_DIM), 0.1),
        "lambda_k1": nrm(ks[9], (DEPTH, HEAD_DIM), 0.1),
        "lambda_q2": nrm(ks[10], (DEPTH, HEAD_DIM), 0.1),
        "lambda_k2": nrm(ks[11], (DEPTH, HEAD_DIM), 0.1),
        "subln_w": gain(ks[12], (DEPTH, 2 * HEAD_DIM)),
        "conv_w": nrm(ks[13], (DEPTH, CONV_WIDTH, D_CONV), CONV_WIDTH ** -0.5),
        "conv_b": nrm(ks[14], (DEPTH, D_CONV), 0.02),
        "conv_ln_g": gain(ks[15], (DEPTH, D_CONV)),
        "conv_ln_b": nrm(ks[16], (DEPTH, D_CONV), 0.02),
        "w_out": nrm(ks[17], (DEPTH, D_MIX, D_MODEL), D_MIX ** -0.5),
        "ffn2_norm": gain(ks[18], (DEPTH, D_MODEL)),
        "ffn2_w_gate": nrm(ks[19], (DEPTH, D_MODEL, D_FF), D_MODEL ** -0.5),
        "ffn2_w_up": nrm(ks[20], (DEPTH, D_MODEL, D_FF), D_MODEL ** -0.5),
        "ffn2_w_down": nrm(ks[21], (DEPTH, D_FF, D_MODEL), D_FF ** -0.5),
        "final_norm": gain(ks[22], (D_MODEL,)),
    }


def reference(x, meta_tokens, ffn1_norm, ffn1_w_gate, ffn1_w_up, ffn1_w_down,
              mix_norm, w_in, lambda_q1, lambda_k1, lambda_q2, lambda_k2, subln_w,
              conv_w, conv_b, conv_ln_g, conv_ln_b, w_out,
              ffn2_norm, ffn2_w_gate, ffn2_w_up, ffn2_w_down, final_norm):
    B = x.shape[0]
    meta = jnp.broadcast_to(meta_tokens.astype(x.dtype)[None], (B, N_META, D_MODEL))
    h_res = jnp.concatenate([meta, x], axis=1)

    for l in range(DEPTH):
        h = rmsnorm(h_res, ffn1_norm[l])
        h_res = h_res + 0.5 * swiglu_ffn(h, ffn1_w_gate[l], ffn1_w_up[l], ffn1_w_down[l])

        h = rmsnorm(h_res, mix_norm[l])
        proj = h @ w_in[l]
        q, k, v, u = jnp.split(proj, [D_ATTN, 2 * D_ATTN, 3 * D_ATTN], axis=-1)
        lambda_init = 0.8 - 0.6 * math.exp(-0.3 * l)
        lam = (jnp.exp(jnp.sum(lambda_q1[l].astype(jnp.float32) * lambda_k1[l].astype(jnp.float32)))
               - jnp.exp(jnp.sum(lambda_q2[l].astype(jnp.float32) * lambda_k2[l].astype(jnp.float32)))
               + lambda_init)
        a = diff_attention(q, k, v, lam, subln_w[l], lambda_init).astype(h_res.dtype)
        c = conformer_conv(u, conv_w[l], conv_b[l], conv_ln_g[l], conv_ln_b[l])
        h_res = h_res + jnp.concatenate([a, c], axis=-1) @ w_out[l]

        h = rmsnorm(h_res, ffn2_norm[l])
        h_res = h_res + 0.5 * swiglu_ffn(h, ffn2_w_gate[l], ffn2_w_up[l], ffn2_w_down[l])

    y = rmsnorm(h_res, final_norm)
    return y[:, N_META:]
```

```python
import contextlib
import numpy as np
import concourse.bass as bass
import concourse.mybir as mybir
from concourse.bass_utils import run_bass_kernel_spmd

F32 = mybir.dt.float32
BF16 = mybir.dt.bfloat16
AF = mybir.ActivationFunctionType
ALU = mybir.AluOpType
AX = mybir.AxisListType

D = 1024
KC = D // 128
EPS = 1e-5
N_META = 16
NEG = -30000.0
RGROUPS = [[0, 1], [2, 3], [4, 5], [6, 7]]


class Buf:
    __slots__ = ("name", "w", "r")

    def __init__(self, name):
        self.name = name
        self.w = None
        self.r = []


class DSem:
    __slots__ = ("key", "count")

    def __init__(self, key):
        self.key = key
        self.count = 0


class Prog:
    ENG = ("pe", "act", "dve", "pool", "sp")

    def __init__(self, tag):
        self.tag = tag
        self.ops = {e: [] for e in self.ENG}
        self.count = {e: 0 for e in self.ENG}
        self.waited = {e: {} for e in self.ENG}
        self.dsems = []

    def dsem(self):
        s = DSem("d%d" % len(self.dsems))
        self.dsems.append(s)
        return s

    def op(self, eng, fn, reads=(), writes=(), dsem=None, ndma=1, cc=False):
        deps = []
        for b in reads:
            if b.w is not None:
                deps.append(b.w)
        for b in writes:
            if b.w is not None:
                deps.append(b.w)
            deps.extend(b.r)
        waits = {}
        wd = self.waited[eng]
        for (sk, v) in deps:
            if sk == eng and eng == "pe":
                continue
            if wd.get(sk, 0) >= v:
                continue
            if waits.get(sk, 0) < v:
                waits[sk] = v
        for sk, v in waits.items():
            wd[sk] = v
        if dsem is None:
            self.count[eng] += 1
            ev = (eng, self.count[eng])
            inc = None
        elif cc:
            dsem.count += 1
            ev = (dsem.key, dsem.count)
            inc = ("cc", dsem.key)
        else:
            dsem.count += 16 * ndma
            ev = (dsem.key, dsem.count)
            inc = dsem.key
        for b in reads:
            b.r.append(ev)
        for b in writes:
            b.w = ev
            b.r = []
        self.ops[eng].append((waits, fn, inc))

    def drain(self):
        waits = {}
        for s in self.dsems:
            if s.count > self.waited["sp"].get(s.key, 0):
                waits[s.key] = s.count
        self.ops["sp"].append((waits, None, None))

    def emit(self, nc):
        with contextlib.ExitStack() as st:
            sems = {}
            for e in self.ENG:
                sems[e] = st.enter_context(nc.semaphore("%s_%s" % (self.tag, e)))
            for s in self.dsems:
                sems[s.key] = st.enter_context(nc.semaphore("%s_%s" % (self.tag, s.key)))
            blk = st.enter_context(nc.Block())

            def replay(e, name):
                for (waits, fn, inc) in self.ops[name]:
                    for sk, v in waits.items():
                        e.wait_ge(sems[sk], v)
                    if fn is None:
                        continue
                    r = fn(e)
                    if inc is None:
                        r.then_inc(sems[name], 1)
                    elif isinstance(inc, tuple):
                        r.then_inc(sems[inc[1]])
                    else:
                        for ins in r:
                            ins.then_inc(sems[inc], 16)

            @blk.tensor
            def _(e):
                replay(e, "pe")

            @blk.scalar
            def _(e):
                replay(e, "act")

            @blk.vector
            def _(e):
                replay(e, "dve")

            @blk.gpsimd
            def _(e):
                replay(e, "pool")

            @blk.sync
            def _(e):
                replay(e, "sp")


def ap(t, off, dims):
    return bass.AP(t, off, [list(d) for d in dims])


def norm_T1(P, src_ap, src_buf, ss_ap, ms_ap, rstd_ap, stat_buf, junk, junk_buf, nhalf, xs, xs_buf, cbuf):
    P.op("act", lambda e: e.activation(out=junk[:], in_=src_ap, func=AF.Square, accum_out=ss_ap),
         reads=[src_buf], writes=[junk_buf, stat_buf])
    P.op("dve", lambda e: e.tensor_scalar(out=ms_ap, in0=ss_ap, scalar1=1.0 / D, scalar2=EPS,
                                          op0=ALU.mult, op1=ALU.add), reads=[], writes=[stat_buf])
    P.op("pool", lambda e: e.tensor_tensor(out=rstd_ap, in0=ms_ap, in1=nhalf[:, 0:1], op=ALU.pow),
         reads=[cbuf], writes=[stat_buf])
    P.op("dve", lambda e: e.tensor_scalar(out=xs[:], in0=src_ap, scalar1=rstd_ap, scalar2=None,
                                          op0=ALU.mult), reads=[src_buf, stat_buf], writes=[xs_buf])


def norm_T2(P, xs, xs_buf, tp, tp_buf, ident, gT, hT, hT_buf, hT_cols, col0, cbuf):
    def tr(e):
        r = None
        for kc in range(KC):
            r = e.transpose(tp[:, kc * 128:(kc + 1) * 128], xs[:, kc * 128:(kc + 1) * 128], ident[:])
        return r
    P.op("pe", tr, reads=[xs_buf, cbuf], writes=[tp_buf])
    o = ap(hT, col0, [[KC * hT_cols, 128], [hT_cols, KC], [1, 128]])
    i0 = ap(tp, 0, [[KC * 128, 128], [128, KC], [1, 128]])
    i1 = ap(gT, 0, [[KC, 128], [1, KC], [0, 128]])
    P.op("dve", lambda e: e.tensor_tensor(out=o, in0=i0, in1=i1, op=ALU.mult),
         reads=[tp_buf, cbuf], writes=[hT_buf])


def norm_T(P, src_ap, src_buf, ss_ap, ms_ap, rstd_ap, stat_buf, junk, junk_buf, nhalf, xs, xs_buf,
           tp, tp_buf, ident, gT, hT, hT_buf, hT_cols, col0, cbuf):
    norm_T1(P, src_ap, src_buf, ss_ap, ms_ap, rstd_ap, stat_buf, junk, junk_buf, nhalf, xs, xs_buf, cbuf)
    norm_T2(P, xs, xs_buf, tp, tp_buf, ident, gT, hT, hT_buf, hT_cols, col0, cbuf)


def ffn_phase(nc, tag, src, dst, ntok, wg_d, wu_d, wd_d, gn_d, FF, ident_d, final_g=None, wq="pool", precast=()):
    TT = 256
    NB = TT // 128
    NF = FF // 128
    ntiles = ntok // TT
    P = Prog(tag)
    with contextlib.ExitStack() as st:
        sb = lambda name, shape, dt: st.enter_context(nc.sbuf_tensor("%s_%s" % (tag, name), shape, dt))
        ps = lambda name, dt, n: st.enter_context(nc.psum_tensor("%s_%s" % (tag, name), [128, n], dt))
        Wg = sb("Wg", [128, KC * FF], BF16)
        Wu = sb("Wu", [128, KC * FF], BF16)
        Wd = sb("Wd", [128, NF * D], BF16)
        xt = [sb("xt%d" % i, [128, NB * D], F32) for i in range(2)]
        xs = [sb("xs%d" % i, [128, D], BF16) for i in range(2)]
        hT = [sb("hT%d" % i, [128, KC * TT], BF16) for i in range(2)]
        actT = sb("actT", [128, NF * TT], BF16)
        sg = [sb("sg%d" % i, [128, TT], F32) for i in range(2)]
        junk = sb("junk", [128, D], BF16)
        stat = [sb("stat%d" % i, [128, 8], F32) for i in range(4)]
        gT = sb("gT", [128, KC], F32)
        nhalf = sb("nhalf", [128, 1], F32)
        ident = sb("ident", [128, 128], BF16)
        gfb = sb("gfb", [128, D], F32) if final_g is not None else None
        tp = [ps("tp%d" % i, BF16, 1024) for i in range(2)]
        gu = [ps("gu%d" % i, F32, 512) for i in range(4)]
        dn = [ps("dn%d" % i, F32, 512) for i in range(2)]

        bWg, bWu, bWd = Buf("Wg"), Buf("Wu"), Buf("Wd")
        bxt = [Buf("xt0"), Buf("xt1")]
        bxs = [Buf("xs0"), Buf("xs1")]
        bhT = [Buf("hT0"), Buf("hT1")]
        bact, bjunk, bconst = Buf("actT"), Buf("junk"), Buf("const")
        bsg = [Buf("sg0"), Buf("sg1")]
        bstat = [Buf("st%d" % i) for i in range(4)]
        btp = [Buf("tp0"), Buf("tp1")]
        bgu = [Buf("gu%d" % i) for i in range(4)]
        bdn = [Buf("dn0"), Buf("dn1")]
        s_w = [P.dsem() for _ in range(3)]
        s_c = P.dsem()
        s_x = [P.dsem(), P.dsem()]

        P.op("pool", lambda e: e.memset(nhalf[:], -0.5), writes=[bconst])
        cl = [lambda e: e.dma_start(out=ident[:], in_=ident_d),
              lambda e: e.dma_start(out=gT[:], in_=gn_d.rearrange("(k p) -> p k", p=128),
                                    allow_slow_non_contiguous=True)]
        if final_g is not None:
            cl.append(lambda e: e.dma_start(out=gfb[:], in_=ap(final_g.tensor, 0, [[0, 128], [1, D]])))
        P.op("pool", lambda e: [f(e) for f in cl], writes=[bconst], dsem=s_c, ndma=len(cl))

        def load_x(t):
            s = t % 2
            P.op("sp", lambda e: [e.dma_start(out=xt[s][:].rearrange("p (b d) -> p b d", b=NB),
                                              in_=src[t * TT:(t + 1) * TT, :].rearrange("(b p) d -> p b d", p=128))],
                 writes=[bxt[s]], dsem=s_x[s])

        load_x(0)
        GB = [0, 6, 12, 18, NF]
        NG = len(GB) - 1
        bWgu = [Buf("Wgu%d" % i) for i in range(NG)]
        s_gu = [P.dsem() for _ in range(NG)]
        for gi in range(NG):
            c0, c1 = GB[gi] * 128, GB[gi + 1] * 128

            def f(e, c0=c0, c1=c1):
                rr = []
                for (Wt, wd_) in ((Wg, wg_d), (Wu, wu_d)):
                    rr.append(e.dma_start(out=ap(Wt, c0, [[KC * FF, 128], [FF, KC], [1, c1 - c0]]),
                                          in_=wd_[:, c0:c1].rearrange("(k p) f -> p k f", p=128)))
                return rr
            P.op(wq, f, writes=[bWgu[gi]], dsem=s_gu[gi], ndma=2)
        fc2g = [max(g for g in range(NG) if GB[g] <= fc) for fc in range(NF)]
        P.op(wq, lambda e: [e.dma_start(out=Wd[:, i * D:(i + 1) * D], in_=wd_d[i * 128:(i + 1) * 128, :])
                            for i in range(NF)], writes=[bWd], dsem=s_w[2], ndma=NF)
        if precast:
            s_pc = P.dsem()
            P.op("pool", lambda e: [e.dma_start(out=o_, in_=i_) for (i_, o_) in precast], dsem=s_pc,
                 ndma=len(precast))

        cnt = {"nst": 0}

        def n1(t, b):
            s_ = t % 2
            k = (t * NB + b) % 2
            sti = cnt["nst"] % 4
            cnt["nst"] += 1
            norm_T1(P, xt[s_][:, b * D:(b + 1) * D], bxt[s_], stat[sti][:, 0:1], stat[sti][:, 1:2],
                    stat[sti][:, 2:3], bstat[sti], junk, bjunk, nhalf, xs[k], bxs[k], bconst)

        def n2(t, b):
            s_ = t % 2
            k = (t * NB + b) % 2
            norm_T2(P, xs[k], bxs[k], tp[k], btp[k], ident, gT, hT[s_], bhT[s_], TT, b * 128, bconst)

        for b in range(NB):
            n1(0, b)
            n2(0, b)

        def do_tile(t):
            s = t % 2
            hTs, bhTs = hT[s], bhT[s]
            if t + 1 < ntiles:
                load_x(t + 1)
            for fc in range(NF):
                g = (t * NF + fc) % 4
                q = (t * NF + fc) % 2
                if t + 1 < ntiles:
                    if fc in (2, 4):
                        n1(t + 1, (fc - 2) // 2)
                    if fc in (12, 14):
                        n2(t + 1, (fc - 12) // 2)

                def mm(e, fc=fc, g=g):
                    r = None
                    for (Wt, c0) in ((Wg, 0), (Wu, TT)):
                        for kc in range(KC):
                            r = e.matmul(gu[g][:, c0:c0 + TT],
                                         lhsT=Wt[:, kc * FF + fc * 128: kc * FF + (fc + 1) * 128],
                                         rhs=hTs[:, kc * TT:(kc + 1) * TT], start=(kc == 0), stop=(kc == KC - 1))
                    return r
                P.op("pe", mm, reads=[bWgu[fc2g[fc]], bhTs], writes=[bgu[g]])
                P.op("act", lambda e, g=g, q=q: e.activation(out=sg[q][:], in_=gu[g][:, 0:TT], func=AF.Silu),
                     reads=[bgu[g]], writes=[bsg[q]])
                P.op("dve", lambda e, g=g, q=q, fc=fc: e.tensor_tensor(
                    out=actT[:, fc * TT:(fc + 1) * TT], in0=gu[g][:, TT:2 * TT], in1=sg[q][:], op=ALU.mult),
                    reads=[bgu[g], bsg[q]], writes=[bact])
            for b in range(NB):
                for hf in range(2):
                    k = (t * 4 + b * 2 + hf) % 2

                    def dmm(e, b=b, hf=hf, k=k):
                        r = None
                        for fc in range(NF):
                            r = e.matmul(dn[k][:, :], lhsT=actT[:, fc * TT + b * 128: fc * TT + (b + 1) * 128],
                                         rhs=Wd[:, fc * D + hf * 512: fc * D + (hf + 1) * 512],
                                         start=(fc == 0), stop=(fc == NF - 1))
                        return r
                    P.op("pe", dmm, reads=[bact, bWd], writes=[bdn[k]])
                    xsl = xt[s][:, b * D + hf * 512: b * D + (hf + 1) * 512]
                    P.op("dve", lambda e, k=k, xsl=xsl: e.scalar_tensor_tensor(
                        out=xsl, in0=dn[k][:, :], scalar=0.5, in1=xsl, op0=ALU.mult, op1=ALU.add),
                        reads=[bdn[k]], writes=[bxt[s]])
            if final_g is not None:
                for b in range(NB):
                    sti = cnt["nst"] % 4
                    cnt["nst"] += 1
                    xb_ = xt[s][:, b * D:(b + 1) * D]
                    P.op("act", lambda e, xb_=xb_, sti=sti: e.activation(out=junk[:], in_=xb_, func=AF.Square,
                                                                      accum_out=stat[sti][:, 0:1]),
                         reads=[bxt[s]], writes=[bjunk, bstat[sti]])
                    P.op("dve", lambda e, sti=sti: e.tensor_scalar(out=stat[sti][:, 1:2], in0=stat[sti][:, 0:1],
                                                                   scalar1=1.0 / D, scalar2=EPS, op0=ALU.mult,
                                                                   op1=ALU.add), writes=[bstat[sti]])
                    P.op("pool", lambda e, sti=sti: e.tensor_tensor(out=stat[sti][:, 2:3], in0=stat[sti][:, 1:2],
                                                                    in1=nhalf[:, 0:1], op=ALU.pow),
                         writes=[bstat[sti]])
                    P.op("dve", lambda e, xb_=xb_, sti=sti: e.scalar_tensor_tensor(
                        out=xb_, in0=xb_, scalar=stat[sti][:, 2:3], in1=gfb[:], op0=ALU.mult, op1=ALU.mult),
                        reads=[bstat[sti], bconst], writes=[bxt[s]])
            P.op("sp", lambda e, s=s, t=t: [e.dma_start(
                out=dst[t * TT:(t + 1) * TT, :].rearrange("(b p) d -> p b d", p=128),
                in_=xt[s][:].rearrange("p (b d) -> p b d", b=NB))], reads=[bxt[s]], dsem=s_x[s])

        for t in range(ntiles):
            do_tile(t)
        P.drain()
        P.emit(nc)


def proj_phase(nc, tag, NPAIR, h1_s, win_d, gm_d, cos_d, sin_d, hmask_d, convw_d, convb_d, lng_d, lnb_d,
               ident_d, KTm_s, Vm_s, Q_s, c_s, send, recv):
    NT = 1 + NPAIR
    T = 512
    P = Prog(tag)
    with contextlib.ExitStack() as st:
        sb = lambda name, shape, dt: st.enter_context(nc.sbuf_tensor("%s_%s" % (tag, name), shape, dt))
        ps = lambda name, dt, n: st.enter_context(nc.psum_tensor("%s_%s" % (tag, name), [128, n], dt))
        Win = sb("Win", [128, KC * 2560], BF16)
        Wp = sb("Wp", [128, KC * 1024], BF16)
        dg = sb("dg", [128, 124 * 128], BF16)
        hb = [sb("hb%d" % i, [128, D], F32) for i in range(2)]
        xs = [sb("xs%d" % i, [128, D], BF16) for i in range(2)]
        hT = [sb("hT%d" % i, [128, KC * T], BF16) for i in range(2)]
        junk = sb("junk", [128, D], BF16)
        stat = [sb("stat%d" % i, [128, 8], F32) for i in range(4)]
        gT = sb("gT", [128, KC], F32)
        nhalf = sb("nhalf", [128, 1], F32)
        ident = sb("ident", [128, 128], BF16)
        onesf = sb("onesf", [128, 128], F32)
        cosb = [sb("cos%d" % i, [128, T], F32) for i in range(2)]
        sinb = [sb("sin%d" % i, [128, T], F32) for i in range(2)]
        t1 = [sb("t1_%d" % i, [128, T], F32) for i in range(2)]
        t2 = [sb("t2_%d" % i, [128, T], F32) for i in range(2)]
        KTt = [sb("KTt%d" % i, [128, 4 * T], BF16) for i in range(2)]
        QTt = sb("QTt", [128, 4 * T], BF16)
        Vt = [sb("Vt%d" % i, [128, 4 * 512], BF16) for i in range(2)]
        cTt = sb("cTt", [128, 4 * T], BF16)
        sgm = [sb("sgm%d" % i, [128, T], F32) for i in range(2)]
        zbuf = sb("zbuf", [128, 4 * 544], BF16)
        zhalo = sb("zhalo", [128, 4 * 256], BF16)
        hmask = sb("hmask", [128, 256], F32)
        acc = [sb("acc%d" % i, [128, T], F32) for i in range(4)]
        ysq = [sb("ysq%d" % i, [128, T], F32) for i in range(2)]
        mean = sb("mean", [128, T], F32)
        msq = sb("msq", [128, T], F32)
        var = sb("var", [128, T], F32)
        rstdl = sb("rstdl", [128, T], F32)
        tn = [sb("tn%d" % i, [128, T], F32) for i in range(2)]
        cwT = sb("cwT", [128, 4 * 31], F32)
        cbT = sb("cbT", [128, 4], F32)
        lgT = sb("lgT", [128, 4], F32)
        lbT = sb("lbT", [128, 4], F32)
        tp = [ps("tp%d" % i, BF16, 1024) for i in range(2)]
        pa = [ps("pa%d" % i, F32, 512) for i in range(2)]
        pb = [ps("pb%d" % i, F32, 512) for i in range(2)]
        pv = [ps("pv%d" % i, F32, 512) for i in range(2)]

        bW, bWp, bconst, bdg = Buf("Win"), Buf("Wp"), Buf("const"), Buf("dg")
        bhb = [Buf("hb0"), Buf("hb1")]
        bxs = [Buf("xs0"), Buf("xs1")]
        bhT, bjunk = [Buf("hT0"), Buf("hT1")], Buf("junk")
        bstat = [Buf("st%d" % i) for i in range(4)]
        brope = [Buf("rope0"), Buf("rope1")]
        bt1 = [Buf("t1_0"), Buf("t1_1")]
        bt2 = [Buf("t2_0"), Buf("t2_1")]
        bKTt = [Buf("KTt0"), Buf("KTt1")]
        bQTt = Buf("QTt")
        bVt = [Buf("Vt0"), Buf("Vt1")]
        bcTt = Buf("cTt")
        bsgm = [Buf("sgm0"), Buf("sgm1")]
        bz, bzh = Buf("zbuf"), Buf("zhalo")
        bacc = [Buf("acc%d" % i) for i in range(4)]
        bysq = [Buf("ysq0"), Buf("ysq1")]
        bmean, bmsq, bvar, brstdl = Buf("mean"), Buf("msq"), Buf("var"), Buf("rstdl")
        btn = [Buf("tn0"), Buf("tn1")]
        btp = [Buf("tp0"), Buf("tp1")]
        bpa = [Buf("pa0"), Buf("pa1")]
        bpb = [Buf("pb0"), Buf("pb1")]
        bpv = [Buf("pv0"), Buf("pv1")]
        s_w, s_c = P.dsem(), P.dsem()
        s_hb = [P.dsem(), P.dsem()]
        s_rope = [P.dsem(), P.dsem()]
        s_K = [P.dsem(), P.dsem()]
        s_Q = P.dsem()
        s_V = [P.dsem(), P.dsem()]
        s_cT = P.dsem()
        s_cc = P.dsem()
        bsend = [Buf("send%d" % i) for i in range(NPAIR)]
        brecv = [Buf("recv%d" % i) for i in range(NPAIR)]

        P.op("pool", lambda e: e.memset(nhalf[:], -0.5), writes=[bconst])
        P.op("pool", lambda e: e.memset(onesf[:], 1.0), writes=[bconst])
        P.op("pool", lambda e: e.memset(zbuf[:], 0.0), writes=[bz])
        cl = [lambda e: e.dma_start(out=ident[:], in_=ident_d),
              lambda e: e.dma_start(out=gT[:], in_=gm_d.rearrange("(k p) -> p k", p=128),
                                    allow_slow_non_contiguous=True),
              lambda e: e.dma_start(out=hmask[:], in_=hmask_d),
              lambda e: e.dma_start(out=cbT[:], in_=convb_d.rearrange("(c p) -> p c", p=128),
                                    allow_slow_non_contiguous=True),
              lambda e: e.dma_start(out=lgT[:], in_=lng_d.rearrange("(c p) -> p c", p=128),
                                    allow_slow_non_contiguous=True),
              lambda e: e.dma_start(out=lbT[:], in_=lnb_d.rearrange("(c p) -> p c", p=128),
                                    allow_slow_non_contiguous=True)]
        for c in range(4):
            cl.append(lambda e, c=c: e.dma_start(out=cwT[:, c * 31:(c + 1) * 31],
                                                 in_=convw_d[:, c * 128:(c + 1) * 128].rearrange("j p -> p j"),
                                                 allow_slow_non_contiguous=True))
        P.op("pool", lambda e: [f(e) for f in cl], writes=[bconst], dsem=s_c, ndma=len(cl))
        P.op("sp", lambda e: [e.dma_start(out=Win[:, i * 2560:(i + 1) * 2560], in_=win_d[i * 128:(i + 1) * 128, :])
                              for i in range(KC)], writes=[bW], dsem=s_w, ndma=KC)
        for kc in range(KC):
            for hf in range(2):
                o = ap(Wp, kc * 1024 + hf * 32, [[KC * 1024, 128], [64, 16], [1, 32]])
                i_ = ap(Win, kc * 2560 + (1 - hf) * 32, [[KC * 2560, 128], [64, 16], [1, 32]])
                P.op("dve", lambda e, o=o, i_=i_: e.tensor_copy(out=o, in_=i_), reads=[bW], writes=[bWp])
        def build_dg():
            for c in range(4):
                o = ap(dg, c * 31 * 128, [[124 * 128, 128], [128, 31], [1, 128]])
                i0_ = ap(ident, 0, [[128, 128], [0, 31], [1, 128]])
                i1_ = ap(cwT, c * 31, [[124, 128], [1, 31], [0, 128]])
                P.op("dve", lambda e, o=o, i0_=i0_, i1_=i1_: e.tensor_tensor(out=o, in0=i0_, in1=i1_, op=ALU.mult),
                     reads=[bconst], writes=[bdg])

        cnt = {"blk": 0, "nst": 0}

        def kind(t):
            extra = (t == 0)
            own = (t >= 1)
            return extra, own, t - 1

        def st_norm(t):
            extra, own, p_own = kind(t)
            nb = 3 if extra else 4
            r = t % 2
            P.op("sp", lambda e: [e.dma_start(out=cosb[r][:], in_=cos_d[:, t * T:(t + 1) * T]),
                                  e.dma_start(out=sinb[r][:], in_=sin_d[:, t * T:(t + 1) * T])],
                 writes=[brope[r]], dsem=s_rope[r], ndma=2)
            for b in range(nb):
                k = cnt["blk"] % 2
                cnt["blk"] += 1
                sti = cnt["nst"] % 4
                cnt["nst"] += 1
                row0 = t * T + b * 128
                P.op("sp", lambda e, k=k, row0=row0: [e.dma_start(out=hb[k][:], in_=h1_s[row0:row0 + 128, :])],
                     writes=[bhb[k]], dsem=s_hb[k])
                norm_T(P, hb[k][:], bhb[k], stat[sti][:, 0:1], stat[sti][:, 1:2], stat[sti][:, 2:3], bstat[sti],
                       junk, bjunk, nhalf, xs[k], bxs[k], tp[k], btp[k], ident, gT, hT[r], bhT[r], T, b * 128, bconst)

        def st_proj(t):
            extra, own, p_own = kind(t)
            r = t % 2
            hTt, bhTt = hT[r], bhT[r]
            ncol = 128 if extra else T

            def proj_rope(wcol0, pcol0, dstT, bdst):
                for c in range(4):
                    i = c % 2

                    def mm(e, c=c, i=i):
                        rr = None
                        for (Wt, stride, c0, pt) in ((Win, 2560, wcol0, pa[i]), (Wp, 1024, pcol0, pb[i])):
                            for kc in range(KC):
                                rr = e.matmul(pt[:, 0:ncol],
                                              lhsT=Wt[:, kc * stride + c0 + c * 128: kc * stride + c0 + (c + 1) * 128],
                                              rhs=hTt[:, kc * T: kc * T + ncol], start=(kc == 0), stop=(kc == KC - 1))
                        return rr
                    P.op("pe", mm, reads=[bW, bWp, bhTt], writes=[bpa[i], bpb[i]])
                    P.op("dve", lambda e, i=i: e.tensor_tensor(out=t1[i][:, 0:ncol], in0=pa[i][:, 0:ncol],
                                                               in1=cosb[r][:, 0:ncol], op=ALU.mult),
                         reads=[bpa[i], brope[r]], writes=[bt1[i]])
                    P.op("dve", lambda e, i=i: e.tensor_tensor(out=t2[i][:, 0:ncol], in0=pb[i][:, 0:ncol],
                                                               in1=sinb[r][:, 0:ncol], op=ALU.mult),
                         reads=[bpb[i], brope[r]], writes=[bt2[i]])
                    P.op("pool", lambda e, i=i, c=c: e.tensor_tensor(out=dstT[:, c * T: c * T + ncol],
                                                                     in0=t1[i][:, 0:ncol], in1=t2[i][:, 0:ncol],
                                                                     op=ALU.add),
                         reads=[bt1[i], bt2[i]], writes=[bdst])

            kk = t % 2
            proj_rope(512, 512, KTt[kk], bKTt[kk])
            if extra:
                P.op("sp", lambda e: [e.dma_start(
                    out=ap(KTm_s.tensor, 0, [[4 * 128, 128], [128, 4], [1, 128]]),
                    in_=ap(KTt[kk], 0, [[4 * T, 128], [T, 4], [1, 128]]))], reads=[bKTt[kk]], dsem=s_K[kk])
            else:
                P.op("sp", lambda e: [e.dma_start(out=send[p_own][0:128, :], in_=KTt[kk][:])],
                     reads=[bKTt[kk]], writes=[bsend[p_own]], dsem=s_K[kk])
            nvb = 1 if extra else 4
            for b in range(nvb):
                i = b % 2

                def vmm(e, b=b, i=i):
                    rr = None
                    for kc in range(KC):
                        rr = e.matmul(pv[i][:, :], lhsT=hTt[:, kc * T + b * 128: kc * T + (b + 1) * 128],
                                      rhs=Win[:, kc * 2560 + 1024: kc * 2560 + 1536],
                                      start=(kc == 0), stop=(kc == KC - 1))
                    return rr
                P.op("pe", vmm, reads=[bW, bhTt], writes=[bpv[i]])
                P.op("act", lambda e, b=b, i=i: e.activation(out=Vt[kk][:, b * 512:(b + 1) * 512], in_=pv[i][:, :],
                                                             func=AF.Copy), reads=[bpv[i]], writes=[bVt[kk]])
            if extra:
                P.op("sp", lambda e: [e.dma_start(out=Vm_s, in_=Vt[kk][:, 0:512])], reads=[bVt[kk]], dsem=s_V[kk])
            else:
                P.op("sp", lambda e: [e.dma_start(out=send[p_own][128:256, :], in_=Vt[kk][:])],
                     reads=[bVt[kk]], writes=[bsend[p_own]], dsem=s_V[kk])
                P.op("pool", lambda e: e.collective_compute(
                    "AllGather", ALU.bypass, replica_groups=RGROUPS,
                    ins=[send[p_own].opt()], outs=[recv[p_own].opt()]),
                    reads=[bsend[p_own]], writes=[brecv[p_own]], dsem=s_cc, cc=True)
            if own:
                proj_rope(0, 0, QTt, bQTt)
                P.op("sp", lambda e: [e.dma_start(out=Q_s[p_own], in_=QTt[:])], reads=[bQTt], dsem=s_Q)
            ucol0, ucols = (128, 256) if extra else (0, T)
            for j in range(4):
                i = j % 2

                def umm(e, j=j, i=i):
                    rr = None
                    for (c0, pt) in ((1536 + j * 128, pa[i]), (2048 + j * 128, pb[i])):
                        for kc in range(KC):
                            rr = e.matmul(pt[:, 0:ucols], lhsT=Win[:, kc * 2560 + c0: kc * 2560 + c0 + 128],
                                          rhs=hTt[:, kc * T + ucol0: kc * T + ucol0 + ucols],
                                          start=(kc == 0), stop=(kc == KC - 1))
                    return rr
                P.op("pe", umm, reads=[bW, bhTt], writes=[bpa[i], bpb[i]])
                P.op("act", lambda e, i=i: e.activation(out=sgm[i][:, 0:ucols], in_=pb[i][:, 0:ucols],
                                                        func=AF.Sigmoid), reads=[bpb[i]], writes=[bsgm[i]])
                if extra:
                    P.op("dve", lambda e, i=i, j=j: e.tensor_tensor(out=sgm[i][:, 0:256],
                                                                    in0=pa[i][:, 0:256], in1=sgm[i][:, 0:256],
                                                                    op=ALU.mult),
                         reads=[bpa[i]], writes=[bsgm[i]])
                    P.op("dve", lambda e, i=i, j=j: e.tensor_tensor(out=zhalo[:, j * 256:(j + 1) * 256],
                                                                    in0=sgm[i][:, 0:256], in1=hmask[:],
                                                                    op=ALU.mult), reads=[bsgm[i], bconst], writes=[bzh])
                else:
                    P.op("dve", lambda e, i=i, j=j: e.tensor_tensor(out=zbuf[:, j * 544 + 32: j * 544 + 544],
                                                                    in0=pa[i][:, :], in1=sgm[i][:, :], op=ALU.mult),
                         reads=[bpa[i], bsgm[i]], writes=[bz])

        def st_conv(t):
            extra, own, p_own = kind(t)
            if not own:
                return
            P.op("pool", lambda e: e.tensor_copy(
                out=ap(zbuf, 0, [[4 * 544, 128], [544, 4], [1, 32]]),
                in_=ap(zhalo, p_own * 32, [[4 * 256, 128], [256, 4], [1, 32]])), reads=[bzh], writes=[bz])
            cps = [pa[0], pa[1], pb[0], pb[1]]
            bcps = [bpa[0], bpa[1], bpb[0], bpb[1]]
            for j in range(4):
                def cmm(e, j=j):
                    rr = None
                    for tap in range(31):
                        rr = e.matmul(cps[j][:, :], lhsT=dg[:, (j * 31 + tap) * 128:(j * 31 + tap + 1) * 128],
                                      rhs=zbuf[:, j * 544 + 2 + tap: j * 544 + 2 + tap + T],
                                      start=(tap == 0), stop=(tap == 30))
                    return rr
                P.op("pe", cmm, reads=[bz, bdg], writes=[bcps[j]])
                P.op("act", lambda e, j=j: e.activation(out=acc[j][:], in_=cps[j][:, :], func=AF.Identity,
                                                        bias=cbT[:, j:j + 1]), reads=[bcps[j], bconst],
                     writes=[bacc[j]])
            def s1mm(e):
                rr = None
                for j in range(4):
                    rr = e.matmul(pv[0][:, :], lhsT=onesf[:], rhs=acc[j][:], start=(j == 0), stop=(j == 3))
                return rr
            P.op("pe", s1mm, reads=bacc + [bconst], writes=[bpv[0]])
            for j in range(4):
                i = j % 2
                P.op("act", lambda e, j=j, i=i: e.activation(out=ysq[i][:], in_=acc[j][:], func=AF.Square),
                     reads=[bacc[j]], writes=[bysq[i]])
                P.op("pe", lambda e, j=j, i=i: e.matmul(pv[1][:, :], lhsT=onesf[:], rhs=ysq[i][:],
                                                        start=(j == 0), stop=(j == 3)),
                     reads=[bysq[i]], writes=[bpv[1]])
            P.op("dve", lambda e: e.tensor_scalar(out=mean[:], in0=pv[0][:, :], scalar1=1.0 / 512, scalar2=None,
                                                  op0=ALU.mult), reads=[bpv[0]], writes=[bmean])
            P.op("dve", lambda e: e.tensor_tensor(out=msq[:], in0=mean[:], in1=mean[:], op=ALU.mult),
                 reads=[bmean], writes=[bmsq])
            P.op("dve", lambda e: e.scalar_tensor_tensor(out=var[:], in0=pv[1][:, :], scalar=1.0 / 512, in1=msq[:],
                                                         op0=ALU.mult, op1=ALU.subtract),
                 reads=[bpv[1], bmsq], writes=[bvar])
            P.op("dve", lambda e: e.tensor_scalar(out=var[:], in0=var[:], scalar1=EPS, scalar2=None, op0=ALU.add),
                 writes=[bvar])
            P.op("act", lambda e: e.activation(out=rstdl[:], in_=var[:], func=AF.Sqrt), reads=[bvar], writes=[brstdl])
            P.op("dve", lambda e: e.reciprocal(out=rstdl[:], in_=rstdl[:]), writes=[brstdl])
            for j in range(4):
                i = j % 2
                P.op("dve", lambda e, j=j, i=i: e.tensor_tensor(out=tn[i][:], in0=acc[j][:], in1=mean[:],
                                                                op=ALU.subtract),
                     reads=[bacc[j], bmean], writes=[btn[i]])
                P.op("dve", lambda e, i=i: e.tensor_tensor(out=tn[i][:], in0=tn[i][:], in1=rstdl[:], op=ALU.mult),
                     reads=[brstdl], writes=[btn[i]])
                P.op("act", lambda e, j=j, i=i: e.activation(out=cTt[:, j * T:(j + 1) * T], in_=tn[i][:],
                                                             func=AF.Silu, scale=lgT[:, j:j + 1],
                                                             bias=lbT[:, j:j + 1]),
                     reads=[btn[i], bconst], writes=[bcTt])
            P.op("sp", lambda e: [e.dma_start(out=c_s[p_own], in_=cTt[:])], reads=[bcTt], dsem=s_cT)

        st_norm(0)
        for t in range(NT):
            st_proj(t)
            if t == 0:
                build_dg()
            if t + 1 < NT:
                st_norm(t + 1)
            st_conv(t)
        P.drain()
        P.emit(nc)


def attn_phase(nc, tag, NPAIR, KTm_s, Vm_s, recv, Q_s, c_s, h1_s, h2_s, wout_d, lq1, lk1, lq2, lk2, subw_d,
               wpos_d):
    T = 512
    KTOT = 128 + 2 * NPAIR * T
    NKB = 1 + 2 * NPAIR * 4
    P = Prog(tag)
    with contextlib.ExitStack() as st:
        sb = lambda name, shape, dt: st.enter_context(nc.sbuf_tensor("%s_%s" % (tag, name), shape, dt))
        ps = lambda name, dt, n: st.enter_context(nc.psum_tensor("%s_%s" % (tag, name), [128, n], dt))
        KT = sb("KT", [128, 4 * KTOT], BF16)
        V = sb("V", [128, NKB * 512], BF16)
        Wo = sb("Wo", [128, KC * D], BF16)
        QTs = [sb("QT%d" % i, [128, 4 * T], BF16) for i in range(2)]
        cT = sb("cT", [128, 4 * T], BF16)
        hb = [sb("hb%d" % i, [128, D], F32) for i in range(2)]
        aT = sb("aT", [128, 4 * T], BF16)
        Pm = [sb("P%d" % i, [128, 2 * T], BF16) for i in range(3)]
        rr_ = sb("r", [128, T], F32)
        o1 = sb("o1", [128, T], F32)
        r2 = sb("r2", [128, 2 * T], F32)
        o12 = sb("o12", [128, 2 * T], F32)
        nhalf = sb("nhalf", [128, 512], F32)
        onesf = sb("onesf", [128, 128], F32)
        onesb = sb("onesb", [128, 128], BF16)
        wpos = [sb("wpos%d" % i, [128, 896], BF16) for i in range(2)]
        lam = sb("lam", [128, 4 * 64 + 64 + 8], F32)
        sw = sb("sw", [128, 2], F32)
        S = [ps("S%d" % i, F32, 1024) for i in range(2)]
        OO = ps("OO", F32, 1024)
        LL = ps("LL", F32, 1024)

        bKT = [Buf("KT%d" % i) for i in range(NPAIR + 1)]
        bV = [Buf("V%d" % i) for i in range(NPAIR + 1)]
        bWo, bconst, blam = Buf("Wo"), Buf("const"), Buf("lam")
        bQTs, bcT = [Buf("QT0"), Buf("QT1")], Buf("cT")
        bhb = [Buf("hb0"), Buf("hb1")]
        baT = Buf("aT")
        bP = [Buf("P%d" % i) for i in range(3)]
        bfin = Buf("fin")
        bS = [Buf("S%d" % i) for i in range(2)]
        bO, bL, br2 = Buf("OO"), Buf("LL"), Buf("r2")
        s_c, s_w = P.dsem(), P.dsem()
        s_KT = [P.dsem() for _ in range(NPAIR + 1)]
        s_V = [P.dsem() for _ in range(NPAIR + 1)]
        s_Q, s_cT = [P.dsem(), P.dsem()], P.dsem()
        s_hb = [P.dsem(), P.dsem()]

        P.op("pool", lambda e: e.memset(nhalf[:], -0.5), writes=[bconst])
        P.op("pool", lambda e: e.memset(onesf[:], 1.0), writes=[bconst])
        P.op("pool", lambda e: e.memset(onesb[:], 1.0), writes=[bconst])
        cl = [lambda e: e.dma_start(out=wpos[0][:], in_=wpos_d[0]),
              lambda e: e.dma_start(out=wpos[1][:], in_=wpos_d[1]),
              lambda e: e.dma_start(out=sw[:, 0:1], in_=subw_d.rearrange("(p o) -> p o", o=1))]
        for i, v in enumerate((lq1, lk1, lq2, lk2)):
            cl.append(lambda e, i=i, v=v: e.dma_start(out=lam[:, i * 64:(i + 1) * 64],
                                                      in_=ap(v.tensor, 0, [[0, 128], [1, 64]])))
        P.op("pool", lambda e: [f(e) for f in cl], writes=[bconst], dsem=s_c, ndma=len(cl))
        P.op("sp", lambda e: [e.dma_start(out=Wo[:, i * D:(i + 1) * D], in_=wout_d[i * 128:(i + 1) * 128, :])
                              for i in range(KC)], writes=[bWo], dsem=s_w, ndma=KC)

        def load_q(p):
            P.op("sp", lambda e: [e.dma_start(out=QTs[p % 2][:], in_=Q_s[p])], writes=[bQTs[p % 2]], dsem=s_Q[p % 2])

        def load_c(p):
            P.op("sp", lambda e: [e.dma_start(out=cT[:], in_=c_s[p])], writes=[bcT], dsem=s_cT)

        def load_hb(p, b):
            k = b % 2
            r0 = (1 + p) * T + b * 128
            P.op("sp", lambda e: [e.dma_start(out=hb[k][:], in_=h1_s[r0:r0 + 128, :])], writes=[bhb[k]],
                 dsem=s_hb[k])

        load_q(0)
        P.op("sp", lambda e: [e.dma_start(out=ap(KT, 0, [[4 * KTOT, 128], [KTOT, 4], [1, 128]]),
                                          in_=ap(KTm_s.tensor, 0, [[4 * 128, 128], [128, 4], [1, 128]]))],
             writes=[bKT[0]], dsem=s_KT[0])
        P.op("sp", lambda e: [e.dma_start(out=V[:, 0:512], in_=Vm_s)], writes=[bV[0]], dsem=s_V[0])
        for g in range(1, NPAIR + 1):
            def ldk(e, g=g):
                rr = []
                for r in range(2):
                    c0 = 128 + (2 * (g - 1) + r) * T
                    rr.append(e.dma_start(out=ap(KT, c0, [[4 * KTOT, 128], [KTOT, 4], [1, T]]),
                                          in_=recv[g - 1][r * 256: r * 256 + 128, :].rearrange(
                                              "p (c t) -> p c t", c=4)))
                return rr

            def ldv(e, g=g):
                rr = []
                for r in range(2):
                    b0 = 1 + (2 * (g - 1) + r) * 4
                    rr.append(e.dma_start(out=V[:, b0 * 512:(b0 + 4) * 512],
                                          in_=recv[g - 1][r * 256 + 128: r * 256 + 256, :]))
                return rr
            P.op("sp", ldk, writes=[bKT[g]], dsem=s_KT[g], ndma=2)
            P.op("sp", ldv, writes=[bV[g]], dsem=s_V[g], ndma=2)
        LQ = 4 * 64
        P.op("dve", lambda e: e.tensor_tensor(out=lam[:, LQ:LQ + 64], in0=lam[:, 0:64], in1=lam[:, 64:128],
                                              op=ALU.mult), reads=[bconst], writes=[blam])
        P.op("dve", lambda e: e.tensor_reduce(out=lam[:, LQ + 64:LQ + 65], in_=lam[:, LQ:LQ + 64], axis=AX.X,
                                              op=ALU.add), writes=[blam])
        P.op("dve", lambda e: e.tensor_tensor(out=lam[:, LQ:LQ + 64], in0=lam[:, 128:192], in1=lam[:, 192:256],
                                              op=ALU.mult), writes=[blam])
        P.op("dve", lambda e: e.tensor_reduce(out=lam[:, LQ + 65:LQ + 66], in_=lam[:, LQ:LQ + 64], axis=AX.X,
                                              op=ALU.add), writes=[blam])
        P.op("act", lambda e: e.activation(out=lam[:, LQ + 66:LQ + 68], in_=lam[:, LQ + 64:LQ + 66], func=AF.Exp),
             reads=[blam], writes=[blam])
        P.op("dve", lambda e: e.tensor_tensor(out=lam[:, LQ + 68:LQ + 69], in0=lam[:, LQ + 67:LQ + 68],
                                              in1=lam[:, LQ + 66:LQ + 67], op=ALU.subtract), writes=[blam])
        P.op("dve", lambda e: e.tensor_scalar(out=lam[:, LQ + 69:LQ + 70], in0=lam[:, LQ + 68:LQ + 69],
                                              scalar1=-0.2, scalar2=None, op0=ALU.add), writes=[blam])
        neglam = lam[:, LQ + 69:LQ + 70]
        P.op("dve", lambda e: e.tensor_scalar(out=sw[:, 1:2], in0=sw[:, 0:1], scalar1=0.8, scalar2=None,
                                              op0=ALU.mult), reads=[bconst], writes=[blam])

        cnt = {"s": 0, "p": 0}

        def slot_blocks(p):
            blocks = [(0, 0, 0, N_META, 0, "n")]
            for pos in range(2 * p + 2):
                for j in range(4):
                    kind = "n"
                    if pos == 2 * p:
                        kind = ("w", 0, j)
                    elif pos == 2 * p + 1:
                        kind = ("w", 1, j)
                    blocks.append((1 + pos // 2, 128 + pos * T + j * 128, 1 + pos * 4 + j, 128, 0, kind))
            return blocks

        sblocks = [slot_blocks(p) for p in range(NPAIR)]
        steps = [(p, dh, bi) for p in range(NPAIR) for dh in range(4) for bi in range(len(sblocks[p]))]
        info = {}

        def front(i):
            p, dh, bi = steps[i]
            QT, bQT = QTs[p % 2], bQTs[p % 2]
            g, kc0, vb, nk, q0, kind = sblocks[p][bi]
            si = cnt["s"] % 2
            cnt["s"] += 1
            pi = cnt["p"] % 3
            cnt["p"] += 1
            nq = T - q0
            info[i] = (pi, nq)

            def qk(e):
                rr = None
                for m in range(2):
                    rr = e.matmul(S[si][0:nk, m * T: m * T + nq],
                                  lhsT=KT[m * 64:(m + 1) * 64, dh * KTOT + kc0: dh * KTOT + kc0 + nk],
                                  rhs=QT[m * 64:(m + 1) * 64, dh * T + q0: dh * T + T],
                                  start=True, stop=True)
                return rr
            P.op("pe", qk, reads=[bKT[g], bQT], writes=[bS[si]])
            src = ap(S[si], 0, [[2 * T, nk], [T, 2], [1, nq]])
            dst = ap(Pm[pi], 0, [[2 * T, nk], [T, 2], [1, nq]])
            P.op("act", lambda e: e.activation(out=dst, in_=src, func=AF.Exp, scale=0.125),
                 reads=[bS[si]], writes=[bP[pi]])
            if kind != "n":
                _, wi, j = kind
                dd = ap(Pm[pi], 0, [[2 * T, 128], [T, 2], [1, T]])
                tt = ap(wpos[wi], 384 - 128 * j, [[896, 128], [0, 2], [1, T]])
                P.op("dve", lambda e: e.tensor_tensor(out=dd, in0=dd, in1=tt, op=ALU.mult),
                     reads=[bconst], writes=[bP[pi]])

        def back(i):
            p, dh, bi = steps[i]
            g, kc0, vb, nk, q0, kind = sblocks[p][bi]
            pi, nq = info[i]
            first, last = (bi == 0), (bi == len(sblocks[p]) - 1)

            def pvm(e):
                rr = None
                for m in range(2):
                    rr = e.matmul(OO[:, m * T + q0:(m + 1) * T],
                                  lhsT=V[0:nk, vb * 512 + dh * 128: vb * 512 + (dh + 1) * 128],
                                  rhs=Pm[pi][0:nk, m * T: m * T + nq], start=first, stop=last)
                    rr = e.matmul(LL[:, m * T + q0:(m + 1) * T], lhsT=onesb[0:nk, :],
                                  rhs=Pm[pi][0:nk, m * T: m * T + nq], start=first, stop=last)
                return rr
            P.op("pe", pvm, reads=[bV[g], bP[pi], bconst], writes=[bO, bL])

        def fin_stages(dh):
            def stA():
                P.op("dve", lambda e: e.tensor_copy(out=r2[:], in_=LL[:, :]), reads=[bL], writes=[br2])
                P.op("act", lambda e: e.activation(out=o12[:], in_=OO[:, :], func=AF.Copy), reads=[bO],
                     writes=[bfin])
                P.op("dve", lambda e: e.reciprocal(out=r2[:], in_=r2[:]), writes=[br2])
                P.op("dve", lambda e: e.tensor_tensor(out=o12[:], in0=o12[:], in1=r2[:], op=ALU.mult),
                     reads=[br2], writes=[bfin])
                P.op("dve", lambda e: e.scalar_tensor_tensor(out=o1[:], in0=o12[:, T:2 * T], scalar=neglam,
                                                             in1=o12[:, 0:T], op0=ALU.mult, op1=ALU.add),
                     reads=[blam], writes=[bfin])

            def stB():
                P.op("act", lambda e: e.activation(out=rr_[:], in_=o1[:], func=AF.Square), reads=[bfin],
                     writes=[bfin])

            ssi = {}

            def stC():
                si = cnt["s"] % 2
                cnt["s"] += 1
                ssi["si"] = si
                P.op("pe", lambda e: e.matmul(S[si][:, 0:T], lhsT=onesf[:], rhs=rr_[:], start=True, stop=True),
                     reads=[bfin, bconst], writes=[bS[si]])

            def stD():
                si = ssi["si"]
                P.op("dve", lambda e: e.tensor_scalar(out=rr_[:], in0=S[si][:, 0:T], scalar1=1.0 / 128,
                                                      scalar2=EPS, op0=ALU.mult, op1=ALU.add),
                     reads=[bS[si]], writes=[bfin])

            def stE():
                P.op("act", lambda e: e.activation(out=rr_[:], in_=rr_[:], func=AF.Sqrt), writes=[bfin])

            def stF():
                P.op("dve", lambda e: e.reciprocal(out=rr_[:], in_=rr_[:]), writes=[bfin])
                P.op("dve", lambda e: e.tensor_tensor(out=o1[:], in0=o1[:], in1=rr_[:], op=ALU.mult),
                     writes=[bfin])

            def stG():
                P.op("act", lambda e: e.activation(out=aT[:, dh * T:(dh + 1) * T], in_=o1[:], func=AF.Copy,
                                                   scale=sw[:, 1:2]), reads=[bfin, blam], writes=[baT])
            return [stA, stB, stC, stD, stE, stF, stG]

        def out_proj(p):
            for b in range(4):
                k = b % 2
                si = cnt["s"] % 2
                cnt["s"] += 1

                def omm(e, b=b, si=si):
                    rr = None
                    for hf in range(2):
                        for c in range(8):
                            src = aT if c < 4 else cT
                            cc = c % 4
                            rr = e.matmul(S[si][:, hf * 512:(hf + 1) * 512],
                                          lhsT=src[:, cc * T + b * 128: cc * T + (b + 1) * 128],
                                          rhs=Wo[:, c * D + hf * 512: c * D + (hf + 1) * 512],
                                          start=(c == 0), stop=(c == 7))
                    return rr
                P.op("pe", omm, reads=[baT, bcT, bWo], writes=[bS[si]])
                P.op("dve", lambda e, si=si, k=k: e.tensor_tensor(out=hb[k][:], in0=S[si][:, :], in1=hb[k][:],
                                                                  op=ALU.add),
                     reads=[bS[si]], writes=[bhb[k]])
                r0 = p * T + b * 128
                P.op("sp", lambda e, k=k, r0=r0: [e.dma_start(out=h2_s[r0:r0 + 128, :], in_=hb[k][:])],
                     reads=[bhb[k]], dsem=s_hb[k])
                if b + 2 < 4:
                    load_hb(p, b + 2)
            if p + 1 < NPAIR:
                load_c(p + 1)
                load_hb(p + 1, 0)
                load_hb(p + 1, 1)

        load_c(0)
        load_hb(0, 0)
        load_hb(0, 1)
        pending = []
        nsteps = len(steps)
        front(0)
        front(1)
        for i in range(nsteps):
            p, dh, bi = steps[i]
            if dh == 0 and bi == 0 and p + 1 < NPAIR:
                load_q(p + 1)
            if i + 2 < nsteps:
                front(i + 2)
            back(i)
            if bi == len(sblocks[p]) - 1:
                while pending:
                    pending.pop(0)[1]()
                st_ = fin_stages(dh)
                st_[0]()
                for k_, f_ in enumerate(st_[1:]):
                    pending.append((i + 9 + k_, f_))
                if dh == 3:
                    pending.append((i + 15, lambda p=p: out_proj(p)))
            while pending and pending[0][0] <= i:
                pending.pop(0)[1]()
        for (_, f_) in pending:
            f_()
        P.drain()
        P.emit(nc)


def build_nc(NPAIR, FF):
    NT = 1 + NPAIR
    T = 512
    nc = bass.Bass("TRN2", target_bir_lowering=False)
    di = lambda n, s: nc.dram_tensor(n, s, F32, kind="ExternalInput").ap()
    xs = di("xs", [NT * T, D])
    cos_t = di("cos_t", [128, NT * T])
    sin_t = di("sin_t", [128, NT * T])
    hmask = di("hmask", [128, 256])
    wpos = di("wpos", [2, 128, 896])
    ident = di("ident", [128, 128])
    g1 = di("ffn1_norm", [D])
    wg1 = di("ffn1_w_gate", [D, FF])
    wu1 = di("ffn1_w_up", [D, FF])
    wd1 = di("ffn1_w_down", [FF, D])
    gm = di("mix_norm", [D])
    win = di("w_in", [D, 2560])
    lq1 = di("lambda_q1", [64])
    lk1 = di("lambda_k1", [64])
    lq2 = di("lambda_q2", [64])
    lk2 = di("lambda_k2", [64])
    subw = di("subln_w", [128])
    convw = di("conv_w", [31, 512])
    convb = di("conv_b", [512])
    lng = di("conv_ln_g", [512])
    lnb = di("conv_ln_b", [512])
    wout = di("w_out", [D, D])
    g2 = di("ffn2_norm", [D])
    wg2 = di("ffn2_w_gate", [D, FF])
    wu2 = di("ffn2_w_up", [D, FF])
    wd2 = di("ffn2_w_down", [FF, D])
    gf = di("final_norm", [D])
    out = nc.dram_tensor("out", [NPAIR * T, D], F32, kind="ExternalOutput").ap()
    h1_s = nc.dram_tensor("h1_s", [NT * T, D], F32, kind="Internal").ap()
    h2_s = nc.dram_tensor("h2_s", [NPAIR * T, D], F32, kind="Internal").ap()
    KTm_s = nc.dram_tensor("KTm_s", [128, 4 * 128], BF16, kind="Internal").ap()
    Vm_s = nc.dram_tensor("Vm_s", [128, 512], BF16, kind="Internal").ap()
    Q_s = nc.dram_tensor("Q_s", [NPAIR, 128, 4 * T], BF16, kind="Internal").ap()
    c_s = nc.dram_tensor("c_s", [NPAIR, 128, 4 * T], BF16, kind="Internal").ap()
    send = [nc.dram_tensor("send%d" % i, [256, 2048], BF16).ap() for i in range(NPAIR)]
    recv = [nc.dram_tensor("recv%d" % i, [512, 2048], BF16).ap() for i in range(NPAIR)]

    sc = lambda n, shp: nc.dram_tensor(n, shp, BF16, kind="Internal").ap()
    wg2b, wu2b, wd2b = sc("wg2b", [D, FF]), sc("wu2b", [D, FF]), sc("wd2b", [FF, D])
    winb, woutb = sc("winb", [D, 2560]), sc("woutb", [D, D])
    pc = [(win, winb), (wout, woutb), (wg2, wg2b), (wu2, wu2b), (wd2, wd2b)]
    ffn_phase(nc, "a1", xs, h1_s, NT * T, wg1, wu1, wd1, g1, FF, ident, precast=pc)
    proj_phase(nc, "a2", NPAIR, h1_s, winb, gm, cos_t, sin_t, hmask, convw, convb, lng, lnb, ident,
               KTm_s, Vm_s, Q_s, c_s, send, recv)
    attn_phase(nc, "b", NPAIR, KTm_s, Vm_s, recv, Q_s, c_s, h1_s, h2_s, woutb, lq1, lk1, lq2, lk2, subw, wpos)
    ffn_phase(nc, "c", h2_s, out, NPAIR * T, wg2b, wu2b, wd2b, g2, FF, ident, final_g=gf, wq="sp")
    return nc


def host_layout(x, meta_tokens, NPAIR):
    B, SEQ, _ = x.shape
    T = 512
    NT = 1 + NPAIR
    assert SEQ == 2 * NPAIR * T
    inv_freq = (np.float32(10000.0) ** (-np.arange(0, 64, 2, dtype=np.float32) / np.float32(64))).astype(np.float32)
    pidx = np.arange(128) % 32
    sgn = np.where((np.arange(128) % 64) < 32, -1.0, 1.0).astype(np.float32)
    tri = (np.arange(128)[:, None] <= np.arange(128)[None, :]).astype(np.float32)
    wdiag = np.concatenate([np.zeros((128, 384), np.float32), tri, np.ones((128, 384), np.float32)], axis=1)
    ident = np.eye(128, dtype=np.float32)
    per_core = []
    for core in range(2 * B):
        b, role = core // 2, core % 2
        xs = np.zeros((NT * T, D), np.float32)
        pos = np.zeros((NT * T,), np.float32)
        hmask = np.ones((128, 256), np.float32)
        xs[0:N_META] = meta_tokens
        pos[0:N_META] = np.arange(N_META)
        for p in range(NPAIR):
            oc = 2 * p + role
            r0 = 128 + p * 32
            if oc == 0:
                xs[r0 + 16: r0 + 32] = meta_tokens
                hmask[:, p * 32: p * 32 + 16] = 0.0
            else:
                xs[r0: r0 + 32] = x[b, oc * T - 32: oc * T]
            tile = 1 + p
            xs[tile * T:(tile + 1) * T] = x[b, oc * T:(oc + 1) * T]
            pos[tile * T:(tile + 1) * T] = N_META + oc * T + np.arange(T)
        ang = pos[None, :].astype(np.float32) * inv_freq[pidx][:, None]
        cos_t = np.cos(ang).astype(np.float32)
        sin_t = (np.sin(ang) * sgn[:, None]).astype(np.float32)
        if role == 0:
            wpos = np.stack([wdiag, np.zeros_like(wdiag)])
        else:
            wpos = np.stack([np.ones_like(wdiag), wdiag])
        per_core.append(dict(xs=xs, cos_t=cos_t, sin_t=sin_t, hmask=hmask, wpos=np.ascontiguousarray(wpos),
                             ident=ident))
    return per_core


WNAMES = ["ffn1_norm", "ffn1_w_gate", "ffn1_w_up", "ffn1_w_down", "mix_norm", "w_in", "lambda_q1", "lambda_k1",
          "lambda_q2", "lambda_k2", "subln_w", "conv_w", "conv_b", "conv_ln_g", "conv_ln_b", "w_out", "ffn2_norm",
          "ffn2_w_gate", "ffn2_w_up", "ffn2_w_down"]


def kernel(**inputs):
    x = np.asarray(inputs["x"], np.float32)
    B, SEQ, _ = x.shape
    T = 512
    NPAIR = SEQ // (2 * T)
    FF = inputs["ffn1_w_gate"].shape[-1]
    meta = np.asarray(inputs["meta_tokens"], np.float32)
    per_core = host_layout(x, meta, NPAIR)
    shared = {n: np.ascontiguousarray(np.asarray(inputs[n], np.float32)[0]) for n in WNAMES}
    shared["final_norm"] = np.ascontiguousarray(np.asarray(inputs["final_norm"], np.float32))
    in_maps = []
    for c in range(2 * B):
        m = dict(shared)
        m.update(per_core[c])
        in_maps.append(m)
    global RGROUPS
    RGROUPS = [[2 * i, 2 * i + 1] for i in range(B)]
    nc = build_nc(NPAIR, FF)
    res = run_bass_kernel_spmd(nc, in_maps, core_ids=list(range(2 * B)))
    out = np.zeros((B, SEQ, D), np.float32)
    for c in range(2 * B):
        b, role = c // 2, c % 2
        o = np.asarray(res.results[c]["out"])
        for p in range(NPAIR):
            oc = 2 * p + role
            out[b, oc * T:(oc + 1) * T] = o[p * T:(p + 1) * T]
    return out
```

```python
import contextlib
import numpy as np
import concourse.bass as bass
import concourse.mybir as mybir
from concourse.bass_utils import run_bass_kernel_spmd

F32 = mybir.dt.float32
BF16 = mybir.dt.bfloat16
AF = mybir.ActivationFunctionType
ALU = mybir.AluOpType
AX = mybir.AxisListType

D = 1024
KC = D // 128
EPS = 1e-5
N_META = 16
NEG = -30000.0
RGROUPS = [[0, 1], [2, 3], [4, 5], [6, 7]]


class Buf:
    __slots__ = ("name", "w", "r")

    def __init__(self, name):
        self.name = name
        self.w = None
        self.r = []


class DSem:
    __slots__ = ("key", "count")

    def __init__(self, key):
        self.key = key
        self.count = 0


class Prog:
    ENG = ("pe", "act", "dve", "pool", "sp")

    def __init__(self, tag):
        self.tag = tag
        self.ops = {e: [] for e in self.ENG}
        self.count = {e: 0 for e in self.ENG}
        self.waited = {e: {} for e in self.ENG}
        self.dsems = []

    def dsem(self):
        s = DSem("d%d" % len(self.dsems))
        self.dsems.append(s)
        return s

    def op(self, eng, fn, reads=(), writes=(), dsem=None, ndma=1, cc=False):
        deps = []
        for b in reads:
            if b.w is not None:
                deps.append(b.w)
        for b in writes:
            if b.w is not None:
                deps.append(b.w)
            deps.extend(b.r)
        waits = {}
        wd = self.waited[eng]
        for (sk, v) in deps:
            if sk == eng and eng == "pe":
                continue
            if wd.get(sk, 0) >= v:
                continue
            if waits.get(sk, 0) < v:
                waits[sk] = v
        for sk, v in waits.items():
            wd[sk] = v
        if dsem is None:
            self.count[eng] += 1
            ev = (eng, self.count[eng])
            inc = None
        elif cc:
            dsem.count += 1
            ev = (dsem.key, dsem.count)
            inc = ("cc", dsem.key)
        else:
            dsem.count += 16 * ndma
            ev = (dsem.key, dsem.count)
            inc = dsem.key
        for b in reads:
            b.r.append(ev)
        for b in writes:
            b.w = ev
            b.r = []
        self.ops[eng].append((waits, fn, inc))

    def drain(self):
        waits = {}
        for s in self.dsems:
            if s.count > self.waited["sp"].get(s.key, 0):
                waits[s.key] = s.count
        self.ops["sp"].append((waits, None, None))

    def emit(self, nc):
        with contextlib.ExitStack() as st:
            sems = {}
            for e in self.ENG:
                sems[e] = st.enter_context(nc.semaphore("%s_%s" % (self.tag, e)))
            for s in self.dsems:
                sems[s.key] = st.enter_context(nc.semaphore("%s_%s" % (self.tag, s.key)))
            blk = st.enter_context(nc.Block())

            def replay(e, name):
                for (waits, fn, inc) in self.ops[name]:
                    for sk, v in waits.items():
                        e.wait_ge(sems[sk], v)
                    if fn is None:
                        continue
                    r = fn(e)
                    if inc is None:
                        r.then_inc(sems[name], 1)
                    elif isinstance(inc, tuple):
                        r.then_inc(sems[inc[1]])
                    else:
                        for ins in r:
                            ins.then_inc(sems[inc], 16)

            @blk.tensor
            def _(e):
                replay(e, "pe")

            @blk.scalar
            def _(e):
                replay(e, "act")

            @blk.vector
            def _(e):
                replay(e, "dve")

            @blk.gpsimd
            def _(e):
                replay(e, "pool")

            @blk.sync
            def _(e):
                replay(e, "sp")


def ap(t, off, dims):
    return bass.AP(t, off, [list(d) for d in dims])


def norm_T1(P, src_ap, src_buf, ss_ap, ms_ap, rstd_ap, stat_buf, junk, junk_buf, nhalf, xs, xs_buf, cbuf):
    P.op("act", lambda e: e.activation(out=junk[:], in_=src_ap, func=AF.Square, accum_out=ss_ap),
         reads=[src_buf], writes=[junk_buf, stat_buf])
    P.op("dve", lambda e: e.tensor_scalar(out=ms_ap, in0=ss_ap, scalar1=1.0 / D, scalar2=EPS,
                                          op0=ALU.mult, op1=ALU.add), reads=[], writes=[stat_buf])
    P.op("pool", lambda e: e.tensor_tensor(out=rstd_ap, in0=ms_ap, in1=nhalf[:, 0:1], op=ALU.pow),
         reads=[cbuf], writes=[stat_buf])
    P.op("dve", lambda e: e.tensor_scalar(out=xs[:], in0=src_ap, scalar1=rstd_ap, scalar2=None,
                                          op0=ALU.mult), reads=[src_buf, stat_buf], writes=[xs_buf])


def norm_T2(P, xs, xs_buf, tp, tp_buf, ident, gT, hT, hT_buf, hT_cols, col0, cbuf):
    def tr(e):
        r = None
        for kc in range(KC):
            r = e.transpose(tp[:, kc * 128:(kc + 1) * 128], xs[:, kc * 128:(kc + 1) * 128], ident[:])
        return r
    P.op("pe", tr, reads=[xs_buf, cbuf], writes=[tp_buf])
    o = ap(hT, col0, [[KC * hT_cols, 128], [hT_cols, KC], [1, 128]])
    i0 = ap(tp, 0, [[KC * 128, 128], [128, KC], [1, 128]])
    i1 = ap(gT, 0, [[KC, 128], [1, KC], [0, 128]])
    P.op("dve", lambda e: e.tensor_tensor(out=o, in0=i0, in1=i1, op=ALU.mult),
         reads=[tp_buf, cbuf], writes=[hT_buf])


def norm_T(P, src_ap, src_buf, ss_ap, ms_ap, rstd_ap, stat_buf, junk, junk_buf, nhalf, xs, xs_buf,
           tp, tp_buf, ident, gT, hT, hT_buf, hT_cols, col0, cbuf):
    norm_T1(P, src_ap, src_buf, ss_ap, ms_ap, rstd_ap, stat_buf, junk, junk_buf, nhalf, xs, xs_buf, cbuf)
    norm_T2(P, xs, xs_buf, tp, tp_buf, ident, gT, hT, hT_buf, hT_cols, col0, cbuf)


def ffn_phase(nc, tag, src, dst, ntok, wg_d, wu_d, wd_d, gn_d, FF, ident_d, final_g=None, wq="pool", precast=()):
    TT = 256
    NB = TT // 128
    NF = FF // 128
    ntiles = ntok // TT
    P = Prog(tag)
    with contextlib.ExitStack() as st:
        sb = lambda name, shape, dt: st.enter_context(nc.sbuf_tensor("%s_%s" % (tag, name), shape, dt))
        ps = lambda name, dt, n: st.enter_context(nc.psum_tensor("%s_%s" % (tag, name), [128, n], dt))
        Wg = sb("Wg", [128, KC * FF], BF16)
        Wu = sb("Wu", [128, KC * FF], BF16)
        Wd = sb("Wd", [128, NF * D], BF16)
        xt = [sb("xt%d" % i, [128, NB * D], F32) for i in range(2)]
        xs = [sb("xs%d" % i, [128, D], BF16) for i in range(2)]
        hT = [sb("hT%d" % i, [128, KC * TT], BF16) for i in range(2)]
        actT = sb("actT", [128, NF * TT], BF16)
        sg = [sb("sg%d" % i, [128, TT], F32) for i in range(2)]
        junk = sb("junk", [128, D], BF16)
        stat = [sb("stat%d" % i, [128, 8], F32) for i in range(4)]
        gT = sb("gT", [128, KC], F32)
        nhalf = sb("nhalf", [128, 1], F32)
        ident = sb("ident", [128, 128], BF16)
        gfb = sb("gfb", [128, D], F32) if final_g is not None else None
        tp = [ps("tp%d" % i, BF16, 1024) for i in range(2)]
        gu = [ps("gu%d" % i, F32, 512) for i in range(4)]
        dn = [ps("dn%d" % i, F32, 512) for i in range(2)]

        bWg, bWu, bWd = Buf("Wg"), Buf("Wu"), Buf("Wd")
        bxt = [Buf("xt0"), Buf("xt1")]
        bxs = [Buf("xs0"), Buf("xs1")]
        bhT = [Buf("hT0"), Buf("hT1")]
        bact, bjunk, bconst = Buf("actT"), Buf("junk"), Buf("const")
        bsg = [Buf("sg0"), Buf("sg1")]
        bstat = [Buf("st%d" % i) for i in range(4)]
        btp = [Buf("tp0"), Buf("tp1")]
        bgu = [Buf("gu%d" % i) for i in range(4)]
        bdn = [Buf("dn0"), Buf("dn1")]
        s_w = [P.dsem() for _ in range(3)]
        s_c = P.dsem()
        s_x = [P.dsem(), P.dsem()]

        P.op("pool", lambda e: e.memset(nhalf[:], -0.5), writes=[bconst])
        cl = [lambda e: e.dma_start(out=ident[:], in_=ident_d),
              lambda e: e.dma_start(out=gT[:], in_=gn_d.rearrange("(k p) -> p k", p=128),
                                    allow_slow_non_contiguous=True)]
        if final_g is not None:
            cl.append(lambda e: e.dma_start(out=gfb[:], in_=ap(final_g.tensor, 0, [[0, 128], [1, D]])))
        P.op("pool", lambda e: [f(e) for f in cl], writes=[bconst], dsem=s_c, ndma=len(cl))

        def load_x(t):
            s = t % 2
            P.op("sp", lambda e: [e.dma_start(out=xt[s][:].rearrange("p (b d) -> p b d", b=NB),
                                              in_=src[t * TT:(t + 1) * TT, :].rearrange("(b p) d -> p b d", p=128))],
                 writes=[bxt[s]], dsem=s_x[s])

        load_x(0)
        GB = [0, 6, 12, 18, NF]
        NG = len(GB) - 1
        bWgu = [Buf("Wgu%d" % i) for i in range(NG)]
        s_gu = [P.dsem() for _ in range(NG)]
        for gi in range(NG):
            c0, c1 = GB[gi] * 128, GB[gi + 1] * 128

            def f(e, c0=c0, c1=c1):
                rr = []
                for (Wt, wd_) in ((Wg, wg_d), (Wu, wu_d)):
                    rr.append(e.dma_start(out=ap(Wt, c0, [[KC * FF, 128], [FF, KC], [1, c1 - c0]]),
                                          in_=wd_[:, c0:c1].rearrange("(k p) f -> p k f", p=128)))
                return rr
            P.op(wq, f, writes=[bWgu[gi]], dsem=s_gu[gi], ndma=2)
        fc2g = [max(g for g in range(NG) if GB[g] <= fc) for fc in range(NF)]
        P.op(wq, lambda e: [e.dma_start(out=Wd[:, i * D:(i + 1) * D], in_=wd_d[i * 128:(i + 1) * 128, :])
                            for i in range(NF)], writes=[bWd], dsem=s_w[2], ndma=NF)
        s_pc = P.dsem() if precast else None

        cnt = {"nst": 0}

        def n1(t, b):
            s_ = t % 2
            k = (t * NB + b) % 2
            sti = cnt["nst"] % 4
            cnt["nst"] += 1
            norm_T1(P, xt[s_][:, b * D:(b + 1) * D], bxt[s_], stat[sti][:, 0:1], stat[sti][:, 1:2],
                    stat[sti][:, 2:3], bstat[sti], junk, bjunk, nhalf, xs[k], bxs[k], bconst)

        def n2(t, b):
            s_ = t % 2
            k = (t * NB + b) % 2
            norm_T2(P, xs[k], bxs[k], tp[k], btp[k], ident, gT, hT[s_], bhT[s_], TT, b * 128, bconst)

        for b in range(NB):
            n1(0, b)
            n2(0, b)

        def do_tile(t):
            s = t % 2
            hTs, bhTs = hT[s], bhT[s]
            if t + 1 < ntiles:
                load_x(t + 1)
            t0p = max(0, min(3, ntiles - len(precast)))
            if precast and t0p <= t < t0p + len(precast):
                i_, o_ = precast[t - t0p]
                P.op("pool", lambda e: [e.dma_start(out=o_, in_=i_)], dsem=s_pc)
            for fc in range(NF):
                g = (t * NF + fc) % 4
                q = (t * NF + fc) % 2
                if t + 1 < ntiles:
                    if fc in (2, 4):
                        n1(t + 1, (fc - 2) // 2)
                    if fc in (12, 14):
                        n2(t + 1, (fc - 12) // 2)

                def mm(e, fc=fc, g=g):
                    r = None
                    for (Wt, c0) in ((Wg, 0), (Wu, TT)):
                        for kc in range(KC):
                            r = e.matmul(gu[g][:, c0:c0 + TT],
                                         lhsT=Wt[:, kc * FF + fc * 128: kc * FF + (fc + 1) * 128],
                                         rhs=hTs[:, kc * TT:(kc + 1) * TT], start=(kc == 0), stop=(kc == KC - 1))
                    return r
                P.op("pe", mm, reads=[bWgu[fc2g[fc]], bhTs], writes=[bgu[g]])
                P.op("act", lambda e, g=g, q=q: e.activation(out=sg[q][:], in_=gu[g][:, 0:TT], func=AF.Silu),
                     reads=[bgu[g]], writes=[bsg[q]])
                P.op("dve", lambda e, g=g, q=q, fc=fc: e.tensor_tensor(
                    out=actT[:, fc * TT:(fc + 1) * TT], in0=gu[g][:, TT:2 * TT], in1=sg[q][:], op=ALU.mult),
                    reads=[bgu[g], bsg[q]], writes=[bact])
            for b in range(NB):
                for hf in range(2):
                    k = (t * 4 + b * 2 + hf) % 2

                    def dmm(e, b=b, hf=hf, k=k):
                        r = None
                        for fc in range(NF):
                            r = e.matmul(dn[k][:, :], lhsT=actT[:, fc * TT + b * 128: fc * TT + (b + 1) * 128],
                                         rhs=Wd[:, fc * D + hf * 512: fc * D + (hf + 1) * 512],
                                         start=(fc == 0), stop=(fc == NF - 1))
                        return r
                    P.op("pe", dmm, reads=[bact, bWd], writes=[bdn[k]])
                    xsl = xt[s][:, b * D + hf * 512: b * D + (hf + 1) * 512]
                    P.op("dve", lambda e, k=k, xsl=xsl: e.scalar_tensor_tensor(
                        out=xsl, in0=dn[k][:, :], scalar=0.5, in1=xsl, op0=ALU.mult, op1=ALU.add),
                        reads=[bdn[k]], writes=[bxt[s]])
            if final_g is not None:
                for b in range(NB):
                    sti = cnt["nst"] % 4
                    cnt["nst"] += 1
                    xb_ = xt[s][:, b * D:(b + 1) * D]
                    P.op("act", lambda e, xb_=xb_, sti=sti: e.activation(out=junk[:], in_=xb_, func=AF.Square,
                                                                      accum_out=stat[sti][:, 0:1]),
                         reads=[bxt[s]], writes=[bjunk, bstat[sti]])
                    P.op("dve", lambda e, sti=sti: e.tensor_scalar(out=stat[sti][:, 1:2], in0=stat[sti][:, 0:1],
                                                                   scalar1=1.0 / D, scalar2=EPS, op0=ALU.mult,
                                                                   op1=ALU.add), writes=[bstat[sti]])
                    P.op("pool", lambda e, sti=sti: e.tensor_tensor(out=stat[sti][:, 2:3], in0=stat[sti][:, 1:2],
                                                                    in1=nhalf[:, 0:1], op=ALU.pow),
                         writes=[bstat[sti]])
                    P.op("dve", lambda e, xb_=xb_, sti=sti: e.scalar_tensor_tensor(
                        out=xb_, in0=xb_, scalar=stat[sti][:, 2:3], in1=gfb[:], op0=ALU.mult, op1=ALU.mult),
                        reads=[bstat[sti], bconst], writes=[bxt[s]])
            P.op("sp", lambda e, s=s, t=t: [e.dma_start(
                out=dst[t * TT:(t + 1) * TT, :].rearrange("(b p) d -> p b d", p=128),
                in_=xt[s][:].rearrange("p (b d) -> p b d", b=NB))], reads=[bxt[s]], dsem=s_x[s])

        for t in range(ntiles):
            do_tile(t)
        P.drain()
        P.emit(nc)


def proj_phase(nc, tag, NPAIR, h1_s, win_d, gm_d, cos_d, sin_d, hmask_d, convw_d, convb_d, lng_d, lnb_d,
               ident_d, KTm_s, Vm_s, Q_s, c_s, send, recv):
    NT = 1 + NPAIR
    T = 512
    P = Prog(tag)
    with contextlib.ExitStack() as st:
        sb = lambda name, shape, dt: st.enter_context(nc.sbuf_tensor("%s_%s" % (tag, name), shape, dt))
        ps = lambda name, dt, n: st.enter_context(nc.psum_tensor("%s_%s" % (tag, name), [128, n], dt))
        Win = sb("Win", [128, KC * 2560], BF16)
        Wp = sb("Wp", [128, KC * 1024], BF16)
        dg = sb("dg", [128, 124 * 128], BF16)
        hb = [sb("hb%d" % i, [128, D], F32) for i in range(2)]
        xs = [sb("xs%d" % i, [128, D], BF16) for i in range(2)]
        hT = [sb("hT%d" % i, [128, KC * T], BF16) for i in range(2)]
        junk = sb("junk", [128, D], BF16)
        stat = [sb("stat%d" % i, [128, 8], F32) for i in range(4)]
        gT = sb("gT", [128, KC], F32)
        nhalf = sb("nhalf", [128, 1], F32)
        ident = sb("ident", [128, 128], BF16)
        onesf = sb("onesf", [128, 128], F32)
        cosb = [sb("cos%d" % i, [128, T], F32) for i in range(2)]
        sinb = [sb("sin%d" % i, [128, T], F32) for i in range(2)]
        t1 = [sb("t1_%d" % i, [128, T], F32) for i in range(2)]
        t2 = [sb("t2_%d" % i, [128, T], F32) for i in range(2)]
        KTt = [sb("KTt%d" % i, [128, 4 * T], BF16) for i in range(2)]
        QTt = sb("QTt", [128, 4 * T], BF16)
        Vt = [sb("Vt%d" % i, [128, 4 * 512], BF16) for i in range(2)]
        cTt = sb("cTt", [128, 4 * T], BF16)
        sgm = [sb("sgm%d" % i, [128, T], F32) for i in range(2)]
        zbuf = sb("zbuf", [128, 4 * 544], BF16)
        zhalo = sb("zhalo", [128, 4 * 256], BF16)
        hmask = sb("hmask", [128, 256], F32)
        acc = [sb("acc%d" % i, [128, T], F32) for i in range(4)]
        ysq = [sb("ysq%d" % i, [128, T], F32) for i in range(2)]
        mean = sb("mean", [128, T], F32)
        msq = sb("msq", [128, T], F32)
        var = sb("var", [128, T], F32)
        rstdl = sb("rstdl", [128, T], F32)
        tn = [sb("tn%d" % i, [128, T], F32) for i in range(2)]
        cwT = sb("cwT", [128, 4 * 31], F32)
        cbT = sb("cbT", [128, 4], F32)
        lgT = sb("lgT", [128, 4], F32)
        lbT = sb("lbT", [128, 4], F32)
        tp = [ps("tp%d" % i, BF16, 1024) for i in range(2)]
        pa = [ps("pa%d" % i, F32, 512) for i in range(2)]
        pb = [ps("pb%d" % i, F32, 512) for i in range(2)]
        pv = [ps("pv%d" % i, F32, 512) for i in range(2)]

        bW, bWp, bconst, bdg = Buf("Win"), Buf("Wp"), Buf("const"), Buf("dg")
        bhb = [Buf("hb0"), Buf("hb1")]
        bxs = [Buf("xs0"), Buf("xs1")]
        bhT, bjunk = [Buf("hT0"), Buf("hT1")], Buf("junk")
        bstat = [Buf("st%d" % i) for i in range(4)]
        brope = [Buf("rope0"), Buf("rope1")]
        bt1 = [Buf("t1_0"), Buf("t1_1")]
        bt2 = [Buf("t2_0"), Buf("t2_1")]
        bKTt = [Buf("KTt0"), Buf("KTt1")]
        bQTt = Buf("QTt")
        bVt = [Buf("Vt0"), Buf("Vt1")]
        bcTt = Buf("cTt")
        bsgm = [Buf("sgm0"), Buf("sgm1")]
        bz, bzh = Buf("zbuf"), Buf("zhalo")
        bacc = [Buf("acc%d" % i) for i in range(4)]
        bysq = [Buf("ysq0"), Buf("ysq1")]
        bmean, bmsq, bvar, brstdl = Buf("mean"), Buf("msq"), Buf("var"), Buf("rstdl")
        btn = [Buf("tn0"), Buf("tn1")]
        btp = [Buf("tp0"), Buf("tp1")]
        bpa = [Buf("pa0"), Buf("pa1")]
        bpb = [Buf("pb0"), Buf("pb1")]
        bpv = [Buf("pv0"), Buf("pv1")]
        s_w, s_c = P.dsem(), P.dsem()
        s_hb = [P.dsem(), P.dsem()]
        s_rope = [P.dsem(), P.dsem()]
        s_K = [P.dsem(), P.dsem()]
        s_Q = P.dsem()
        s_V = [P.dsem(), P.dsem()]
        s_cT = P.dsem()
        s_cc = P.dsem()
        bsend = [Buf("send%d" % i) for i in range(NPAIR)]
        brecv = [Buf("recv%d" % i) for i in range(NPAIR)]

        P.op("pool", lambda e: e.memset(nhalf[:], -0.5), writes=[bconst])
        P.op("pool", lambda e: e.memset(onesf[:], 1.0), writes=[bconst])
        P.op("pool", lambda e: e.memset(zbuf[:], 0.0), writes=[bz])
        cl = [lambda e: e.dma_start(out=ident[:], in_=ident_d),
              lambda e: e.dma_start(out=gT[:], in_=gm_d.rearrange("(k p) -> p k", p=128),
                                    allow_slow_non_contiguous=True),
              lambda e: e.dma_start(out=hmask[:], in_=hmask_d),
              lambda e: e.dma_start(out=cbT[:], in_=convb_d.rearrange("(c p) -> p c", p=128),
                                    allow_slow_non_contiguous=True),
              lambda e: e.dma_start(out=lgT[:], in_=lng_d.rearrange("(c p) -> p c", p=128),
                                    allow_slow_non_contiguous=True),
              lambda e: e.dma_start(out=lbT[:], in_=lnb_d.rearrange("(c p) -> p c", p=128),
                                    allow_slow_non_contiguous=True)]
        for c in range(4):
            cl.append(lambda e, c=c: e.dma_start(out=cwT[:, c * 31:(c + 1) * 31],
                                                 in_=convw_d[:, c * 128:(c + 1) * 128].rearrange("j p -> p j"),
                                                 allow_slow_non_contiguous=True))
        P.op("pool", lambda e: [f(e) for f in cl], writes=[bconst], dsem=s_c, ndma=len(cl))
        P.op("sp", lambda e: [e.dma_start(out=Win[:, i * 2560:(i + 1) * 2560], in_=win_d[i * 128:(i + 1) * 128, :])
                              for i in range(KC)], writes=[bW], dsem=s_w, ndma=KC)
        for kc in range(KC):
            for hf in range(2):
                o = ap(Wp, kc * 1024 + hf * 32, [[KC * 1024, 128], [64, 16], [1, 32]])
                i_ = ap(Win, kc * 2560 + (1 - hf) * 32, [[KC * 2560, 128], [64, 16], [1, 32]])
                P.op("dve", lambda e, o=o, i_=i_: e.tensor_copy(out=o, in_=i_), reads=[bW], writes=[bWp])
        def build_dg():
            for c in range(4):
                o = ap(dg, c * 31 * 128, [[124 * 128, 128], [128, 31], [1, 128]])
                i0_ = ap(ident, 0, [[128, 128], [0, 31], [1, 128]])
                i1_ = ap(cwT, c * 31, [[124, 128], [1, 31], [0, 128]])
                P.op("dve", lambda e, o=o, i0_=i0_, i1_=i1_: e.tensor_tensor(out=o, in0=i0_, in1=i1_, op=ALU.mult),
                     reads=[bconst], writes=[bdg])

        cnt = {"blk": 0, "nst": 0}

        def kind(t):
            extra = (t == 0)
            own = (t >= 1)
            return extra, own, t - 1

        def st_norm(t):
            extra, own, p_own = kind(t)
            nb = 3 if extra else 4
            r = t % 2
            P.op("sp", lambda e: [e.dma_start(out=cosb[r][:], in_=cos_d[:, t * T:(t + 1) * T]),
                                  e.dma_start(out=sinb[r][:], in_=sin_d[:, t * T:(t + 1) * T])],
                 writes=[brope[r]], dsem=s_rope[r], ndma=2)
            for b in range(nb):
                k = cnt["blk"] % 2
                cnt["blk"] += 1
                sti = cnt["nst"] % 4
                cnt["nst"] += 1
                row0 = t * T + b * 128
                P.op("sp", lambda e, k=k, row0=row0: [e.dma_start(out=hb[k][:], in_=h1_s[row0:row0 + 128, :])],
                     writes=[bhb[k]], dsem=s_hb[k])
                norm_T(P, hb[k][:], bhb[k], stat[sti][:, 0:1], stat[sti][:, 1:2], stat[sti][:, 2:3], bstat[sti],
                       junk, bjunk, nhalf, xs[k], bxs[k], tp[k], btp[k], ident, gT, hT[r], bhT[r], T, b * 128, bconst)

        def st_proj(t):
            extra, own, p_own = kind(t)
            r = t % 2
            hTt, bhTt = hT[r], bhT[r]
            ncol = 128 if extra else T

            def proj_rope(wcol0, pcol0, dstT, bdst):
                for c in range(4):
                    i = c % 2

                    def mm(e, c=c, i=i):
                        rr = None
                        for (Wt, stride, c0, pt) in ((Win, 2560, wcol0, pa[i]), (Wp, 1024, pcol0, pb[i])):
                            for kc in range(KC):
                                rr = e.matmul(pt[:, 0:ncol],
                                              lhsT=Wt[:, kc * stride + c0 + c * 128: kc * stride + c0 + (c + 1) * 128],
                                              rhs=hTt[:, kc * T: kc * T + ncol], start=(kc == 0), stop=(kc == KC - 1))
                        return rr
                    P.op("pe", mm, reads=[bW, bWp, bhTt], writes=[bpa[i], bpb[i]])
                    P.op("dve", lambda e, i=i: e.tensor_tensor(out=t1[i][:, 0:ncol], in0=pa[i][:, 0:ncol],
                                                               in1=cosb[r][:, 0:ncol], op=ALU.mult),
                         reads=[bpa[i], brope[r]], writes=[bt1[i]])
                    P.op("dve", lambda e, i=i: e.tensor_tensor(out=t2[i][:, 0:ncol], in0=pb[i][:, 0:ncol],
                                                               in1=sinb[r][:, 0:ncol], op=ALU.mult),
                         reads=[bpb[i], brope[r]], writes=[bt2[i]])
                    P.op("pool", lambda e, i=i, c=c: e.tensor_tensor(out=dstT[:, c * T: c * T + ncol],
                                                                     in0=t1[i][:, 0:ncol], in1=t2[i][:, 0:ncol],
                                                                     op=ALU.add),
                         reads=[bt1[i], bt2[i]], writes=[bdst])

            kk = t % 2
            proj_rope(512, 512, KTt[kk], bKTt[kk])
            if extra:
                P.op("sp", lambda e: [e.dma_start(
                    out=ap(KTm_s.tensor, 0, [[4 * 128, 128], [128, 4], [1, 128]]),
                    in_=ap(KTt[kk], 0, [[4 * T, 128], [T, 4], [1, 128]]))], reads=[bKTt[kk]], dsem=s_K[kk])
            else:
                P.op("sp", lambda e: [e.dma_start(out=send[p_own][0:128, :], in_=KTt[kk][:])],
                     reads=[bKTt[kk]], writes=[bsend[p_own]], dsem=s_K[kk])
            nvb = 1 if extra else 4
            for b in range(nvb):
                i = b % 2

                def vmm(e, b=b, i=i):
                    rr = None
                    for kc in range(KC):
                        rr = e.matmul(pv[i][:, :], lhsT=hTt[:, kc * T + b * 128: kc * T + (b + 1) * 128],
                                      rhs=Win[:, kc * 2560 + 1024: kc * 2560 + 1536],
                                      start=(kc == 0), stop=(kc == KC - 1))
                    return rr
                P.op("pe", vmm, reads=[bW, bhTt], writes=[bpv[i]])
                P.op("act", lambda e, b=b, i=i: e.activation(out=Vt[kk][:, b * 512:(b + 1) * 512], in_=pv[i][:, :],
                                                             func=AF.Copy), reads=[bpv[i]], writes=[bVt[kk]])
            if extra:
                P.op("sp", lambda e: [e.dma_start(out=Vm_s, in_=Vt[kk][:, 0:512])], reads=[bVt[kk]], dsem=s_V[kk])
            else:
                P.op("sp", lambda e: [e.dma_start(out=send[p_own][128:256, :], in_=Vt[kk][:])],
                     reads=[bVt[kk]], writes=[bsend[p_own]], dsem=s_V[kk])
                P.op("pool", lambda e: e.collective_compute(
                    "AllGather", ALU.bypass, replica_groups=RGROUPS,
                    ins=[send[p_own].opt()], outs=[recv[p_own].opt()]),
                    reads=[bsend[p_own]], writes=[brecv[p_own]], dsem=s_cc, cc=True)
            if own:
                proj_rope(0, 0, QTt, bQTt)
                P.op("sp", lambda e: [e.dma_start(out=Q_s[p_own], in_=QTt[:])], reads=[bQTt], dsem=s_Q)
            ucol0, ucols = (128, 256) if extra else (0, T)
            for j in range(4):
                i = j % 2

                def umm(e, j=j, i=i):
                    rr = None
                    for (c0, pt) in ((1536 + j * 128, pa[i]), (2048 + j * 128, pb[i])):
                        for kc in range(KC):
                            rr = e.matmul(pt[:, 0:ucols], lhsT=Win[:, kc * 2560 + c0: kc * 2560 + c0 + 128],
                                          rhs=hTt[:, kc * T + ucol0: kc * T + ucol0 + ucols],
                                          start=(kc == 0), stop=(kc == KC - 1))
                    return rr
                P.op("pe", umm, reads=[bW, bhTt], writes=[bpa[i], bpb[i]])
                P.op("act", lambda e, i=i: e.activation(out=sgm[i][:, 0:ucols], in_=pb[i][:, 0:ucols],
                                                        func=AF.Sigmoid), reads=[bpb[i]], writes=[bsgm[i]])
                if extra:
                    P.op("dve", lambda e, i=i, j=j: e.tensor_tensor(out=sgm[i][:, 0:256],
                                                                    in0=pa[i][:, 0:256], in1=sgm[i][:, 0:256],
                                                                    op=ALU.mult),
                         reads=[bpa[i]], writes=[bsgm[i]])
                    P.op("dve", lambda e, i=i, j=j: e.tensor_tensor(out=zhalo[:, j * 256:(j + 1) * 256],
                                                                    in0=sgm[i][:, 0:256], in1=hmask[:],
                                                                    op=ALU.mult), reads=[bsgm[i], bconst], writes=[bzh])
                else:
                    P.op("dve", lambda e, i=i, j=j: e.tensor_tensor(out=zbuf[:, j * 544 + 32: j * 544 + 544],
                                                                    in0=pa[i][:, :], in1=sgm[i][:, :], op=ALU.mult),
                         reads=[bpa[i], bsgm[i]], writes=[bz])

        def st_conv(t):
            extra, own, p_own = kind(t)
            if not own:
                return
            P.op("pool", lambda e: e.tensor_copy(
                out=ap(zbuf, 0, [[4 * 544, 128], [544, 4], [1, 32]]),
                in_=ap(zhalo, p_own * 32, [[4 * 256, 128], [256, 4], [1, 32]])), reads=[bzh], writes=[bz])
            cps = [pa[0], pa[1], pb[0], pb[1]]
            bcps = [bpa[0], bpa[1], bpb[0], bpb[1]]
            for j in range(4):
                def cmm(e, j=j):
                    rr = None
                    for tap in range(31):
                        rr = e.matmul(cps[j][:, :], lhsT=dg[:, (j * 31 + tap) * 128:(j * 31 + tap + 1) * 128],
                                      rhs=zbuf[:, j * 544 + 2 + tap: j * 544 + 2 + tap + T],
                                      start=(tap == 0), stop=(tap == 30))
                    return rr
                P.op("pe", cmm, reads=[bz, bdg], writes=[bcps[j]])
                P.op("act", lambda e, j=j: e.activation(out=acc[j][:], in_=cps[j][:, :], func=AF.Identity,
                                                        bias=cbT[:, j:j + 1]), reads=[bcps[j], bconst],
                     writes=[bacc[j]])
            def s1mm(e):
                rr = None
                for j in range(4):
                    rr = e.matmul(pv[0][:, :], lhsT=onesf[:], rhs=acc[j][:], start=(j == 0), stop=(j == 3))
                return rr
            P.op("pe", s1mm, reads=bacc + [bconst], writes=[bpv[0]])
            for j in range(4):
                i = j % 2
                P.op("act", lambda e, j=j, i=i: e.activation(out=ysq[i][:], in_=acc[j][:], func=AF.Square),
                     reads=[bacc[j]], writes=[bysq[i]])
                P.op("pe", lambda e, j=j, i=i: e.matmul(pv[1][:, :], lhsT=onesf[:], rhs=ysq[i][:],
                                                        start=(j == 0), stop=(j == 3)),
                     reads=[bysq[i]], writes=[bpv[1]])
            P.op("dve", lambda e: e.tensor_scalar(out=mean[:], in0=pv[0][:, :], scalar1=1.0 / 512, scalar2=None,
                                                  op0=ALU.mult), reads=[bpv[0]], writes=[bmean])
            P.op("dve", lambda e: e.tensor_tensor(out=msq[:], in0=mean[:], in1=mean[:], op=ALU.mult),
                 reads=[bmean], writes=[bmsq])
            P.op("dve", lambda e: e.scalar_tensor_tensor(out=var[:], in0=pv[1][:, :], scalar=1.0 / 512, in1=msq[:],
                                                         op0=ALU.mult, op1=ALU.subtract),
                 reads=[bpv[1], bmsq], writes=[bvar])
            P.op("dve", lambda e: e.tensor_scalar(out=var[:], in0=var[:], scalar1=EPS, scalar2=None, op0=ALU.add),
                 writes=[bvar])
            P.op("act", lambda e: e.activation(out=rstdl[:], in_=var[:], func=AF.Sqrt), reads=[bvar], writes=[brstdl])
            P.op("dve", lambda e: e.reciprocal(out=rstdl[:], in_=rstdl[:]), writes=[brstdl])
            for j in range(4):
                i = j % 2
                P.op("dve", lambda e, j=j, i=i: e.tensor_tensor(out=tn[i][:], in0=acc[j][:], in1=mean[:],
                                                                op=ALU.subtract),
                     reads=[bacc[j], bmean], writes=[btn[i]])
                P.op("dve", lambda e, i=i: e.tensor_tensor(out=tn[i][:], in0=tn[i][:], in1=rstdl[:], op=ALU.mult),
                     reads=[brstdl], writes=[btn[i]])
                P.op("act", lambda e, j=j, i=i: e.activation(out=cTt[:, j * T:(j + 1) * T], in_=tn[i][:],
                                                             func=AF.Silu, scale=lgT[:, j:j + 1],
                                                             bias=lbT[:, j:j + 1]),
                     reads=[btn[i], bconst], writes=[bcTt])
            P.op("sp", lambda e: [e.dma_start(out=c_s[p_own], in_=cTt[:])], reads=[bcTt], dsem=s_cT)

        st_norm(0)
        for t in range(NT):
            st_proj(t)
            if t == 0:
                build_dg()
            if t + 1 < NT:
                st_norm(t + 1)
            st_conv(t)
        P.drain()
        P.emit(nc)


def attn_phase(nc, tag, NPAIR, KTm_s, Vm_s, recv, Q_s, c_s, h1_s, h2_s, wout_d, lq1, lk1, lq2, lk2, subw_d,
               wpos_d):
    T = 512
    KTOT = 128 + 2 * NPAIR * T
    NKB = 1 + 2 * NPAIR * 4
    P = Prog(tag)
    with contextlib.ExitStack() as st:
        sb = lambda name, shape, dt: st.enter_context(nc.sbuf_tensor("%s_%s" % (tag, name), shape, dt))
        ps = lambda name, dt, n: st.enter_context(nc.psum_tensor("%s_%s" % (tag, name), [128, n], dt))
        KT = sb("KT", [128, 4 * KTOT], BF16)
        V = sb("V", [128, NKB * 512], BF16)
        Wo = sb("Wo", [128, KC * D], BF16)
        QTs = [sb("QT%d" % i, [128, 4 * T], BF16) for i in range(2)]
        cT = sb("cT", [128, 4 * T], BF16)
        hb = [sb("hb%d" % i, [128, D], F32) for i in range(2)]
        aT = sb("aT", [128, 4 * T], BF16)
        Pm = [sb("P%d" % i, [128, 2 * T], BF16) for i in range(3)]
        rr_ = sb("r", [128, T], F32)
        o1 = sb("o1", [128, T], F32)
        r2 = sb("r2", [128, 2 * T], F32)
        o12 = sb("o12", [128, 2 * T], F32)
        nhalf = sb("nhalf", [128, 512], F32)
        onesf = sb("onesf", [128, 128], F32)
        onesb = sb("onesb", [128, 128], BF16)
        wpos = [sb("wpos%d" % i, [128, 896], BF16) for i in range(2)]
        lam = sb("lam", [128, 4 * 64 + 64 + 8], F32)
        sw = sb("sw", [128, 2], F32)
        S = [ps("S%d" % i, F32, 1024) for i in range(2)]
        OO = ps("OO", F32, 1024)
        LL = ps("LL", F32, 1024)

        bKT = [Buf("KT%d" % i) for i in range(NPAIR + 1)]
        bV = [Buf("V%d" % i) for i in range(NPAIR + 1)]
        bWo, bconst, blam = Buf("Wo"), Buf("const"), Buf("lam")
        bQTs, bcT = [Buf("QT0"), Buf("QT1")], Buf("cT")
        bhb = [Buf("hb0"), Buf("hb1")]
        baT = Buf("aT")
        bP = [Buf("P%d" % i) for i in range(3)]
        bfin = Buf("fin")
        bS = [Buf("S%d" % i) for i in range(2)]
        bO, bL, br2 = Buf("OO"), Buf("LL"), Buf("r2")
        s_c, s_w = P.dsem(), P.dsem()
        s_KT = [P.dsem() for _ in range(NPAIR + 1)]
        s_V = [P.dsem() for _ in range(NPAIR + 1)]
        s_Q, s_cT = [P.dsem(), P.dsem()], P.dsem()
        s_hb = [P.dsem(), P.dsem()]

        P.op("pool", lambda e: e.memset(nhalf[:], -0.5), writes=[bconst])
        P.op("pool", lambda e: e.memset(onesf[:], 1.0), writes=[bconst])
        P.op("pool", lambda e: e.memset(onesb[:], 1.0), writes=[bconst])
        cl = [lambda e: e.dma_start(out=wpos[0][:], in_=wpos_d[0]),
              lambda e: e.dma_start(out=wpos[1][:], in_=wpos_d[1]),
              lambda e: e.dma_start(out=sw[:, 0:1], in_=subw_d.rearrange("(p o) -> p o", o=1))]
        for i, v in enumerate((lq1, lk1, lq2, lk2)):
            cl.append(lambda e, i=i, v=v: e.dma_start(out=lam[:, i * 64:(i + 1) * 64],
                                                      in_=ap(v.tensor, 0, [[0, 128], [1, 64]])))
        P.op("pool", lambda e: [f(e) for f in cl], writes=[bconst], dsem=s_c, ndma=len(cl))
        P.op("sp", lambda e: [e.dma_start(out=Wo[:, i * D:(i + 1) * D], in_=wout_d[i * 128:(i + 1) * 128, :])
                              for i in range(KC)], writes=[bWo], dsem=s_w, ndma=KC)

        def load_q(p):
            P.op("sp", lambda e: [e.dma_start(out=QTs[p % 2][:], in_=Q_s[p])], writes=[bQTs[p % 2]], dsem=s_Q[p % 2])

        def load_c(p):
            P.op("sp", lambda e: [e.dma_start(out=cT[:], in_=c_s[p])], writes=[bcT], dsem=s_cT)

        def load_hb(p, b):
            k = b % 2
            r0 = (1 + p) * T + b * 128
            P.op("sp", lambda e: [e.dma_start(out=hb[k][:], in_=h1_s[r0:r0 + 128, :])], writes=[bhb[k]],
                 dsem=s_hb[k])

        load_q(0)
        P.op("sp", lambda e: [e.dma_start(out=ap(KT, 0, [[4 * KTOT, 128], [KTOT, 4], [1, 128]]),
                                          in_=ap(KTm_s.tensor, 0, [[4 * 128, 128], [128, 4], [1, 128]]))],
             writes=[bKT[0]], dsem=s_KT[0])
        P.op("sp", lambda e: [e.dma_start(out=V[:, 0:512], in_=Vm_s)], writes=[bV[0]], dsem=s_V[0])
        for g in range(1, NPAIR + 1):
            def ldk(e, g=g):
                rr = []
                for r in range(2):
                    c0 = 128 + (2 * (g - 1) + r) * T
                    rr.append(e.dma_start(out=ap(KT, c0, [[4 * KTOT, 128], [KTOT, 4], [1, T]]),
                                          in_=recv[g - 1][r * 256: r * 256 + 128, :].rearrange(
                                              "p (c t) -> p c t", c=4)))
                return rr

            def ldv(e, g=g):
                rr = []
                for r in range(2):
                    b0 = 1 + (2 * (g - 1) + r) * 4
                    rr.append(e.dma_start(out=V[:, b0 * 512:(b0 + 4) * 512],
                                          in_=recv[g - 1][r * 256 + 128: r * 256 + 256, :]))
                return rr
            P.op("sp", ldk, writes=[bKT[g]], dsem=s_KT[g], ndma=2)
            P.op("sp", ldv, writes=[bV[g]], dsem=s_V[g], ndma=2)
        LQ = 4 * 64
        P.op("dve", lambda e: e.tensor_tensor(out=lam[:, LQ:LQ + 64], in0=lam[:, 0:64], in1=lam[:, 64:128],
                                              op=ALU.mult), reads=[bconst], writes=[blam])
        P.op("dve", lambda e: e.tensor_reduce(out=lam[:, LQ + 64:LQ + 65], in_=lam[:, LQ:LQ + 64], axis=AX.X,
                                              op=ALU.add), writes=[blam])
        P.op("dve", lambda e: e.tensor_tensor(out=lam[:, LQ:LQ + 64], in0=lam[:, 128:192], in1=lam[:, 192:256],
                                              op=ALU.mult), writes=[blam])
        P.op("dve", lambda e: e.tensor_reduce(out=lam[:, LQ + 65:LQ + 66], in_=lam[:, LQ:LQ + 64], axis=AX.X,
                                              op=ALU.add), writes=[blam])
        P.op("act", lambda e: e.activation(out=lam[:, LQ + 66:LQ + 68], in_=lam[:, LQ + 64:LQ + 66], func=AF.Exp),
             reads=[blam], writes=[blam])
        P.op("dve", lambda e: e.tensor_tensor(out=lam[:, LQ + 68:LQ + 69], in0=lam[:, LQ + 67:LQ + 68],
                                              in1=lam[:, LQ + 66:LQ + 67], op=ALU.subtract), writes=[blam])
        P.op("dve", lambda e: e.tensor_scalar(out=lam[:, LQ + 69:LQ + 70], in0=lam[:, LQ + 68:LQ + 69],
                                              scalar1=-0.2, scalar2=None, op0=ALU.add), writes=[blam])
        neglam = lam[:, LQ + 69:LQ + 70]
        P.op("dve", lambda e: e.tensor_scalar(out=sw[:, 1:2], in0=sw[:, 0:1], scalar1=0.8, scalar2=None,
                                              op0=ALU.mult), reads=[bconst], writes=[blam])

        cnt = {"s": 0, "p": 0}

        def slot_blocks(p):
            blocks = [(0, 0, 0, N_META, 0, "n")]
            for pos in range(2 * p + 2):
                for j in range(4):
                    kind = "n"
                    if pos == 2 * p:
                        kind = ("w", 0, j)
                    elif pos == 2 * p + 1:
                        kind = ("w", 1, j)
                    blocks.append((1 + pos // 2, 128 + pos * T + j * 128, 1 + pos * 4 + j, 128, 0, kind))
            return blocks

        sblocks = [slot_blocks(p) for p in range(NPAIR)]
        steps = [(p, dh, bi) for p in range(NPAIR) for dh in range(4) for bi in range(len(sblocks[p]))]
        info = {}

        def front(i):
            p, dh, bi = steps[i]
            QT, bQT = QTs[p % 2], bQTs[p % 2]
            g, kc0, vb, nk, q0, kind = sblocks[p][bi]
            si = cnt["s"] % 2
            cnt["s"] += 1
            pi = cnt["p"] % 3
            cnt["p"] += 1
            nq = T - q0
            info[i] = (pi, nq)

            def qk(e):
                rr = None
                for m in range(2):
                    rr = e.matmul(S[si][0:nk, m * T: m * T + nq],
                                  lhsT=KT[m * 64:(m + 1) * 64, dh * KTOT + kc0: dh * KTOT + kc0 + nk],
                                  rhs=QT[m * 64:(m + 1) * 64, dh * T + q0: dh * T + T],
                                  start=True, stop=True)
                return rr
            P.op("pe", qk, reads=[bKT[g], bQT], writes=[bS[si]])
            src = ap(S[si], 0, [[2 * T, nk], [T, 2], [1, nq]])
            dst = ap(Pm[pi], 0, [[2 * T, nk], [T, 2], [1, nq]])
            P.op("act", lambda e: e.activation(out=dst, in_=src, func=AF.Exp, scale=0.125),
                 reads=[bS[si]], writes=[bP[pi]])
            if kind != "n":
                _, wi, j = kind
                dd = ap(Pm[pi], 0, [[2 * T, 128], [T, 2], [1, T]])
                tt = ap(wpos[wi], 384 - 128 * j, [[896, 128], [0, 2], [1, T]])
                P.op("dve", lambda e: e.tensor_tensor(out=dd, in0=dd, in1=tt, op=ALU.mult),
                     reads=[bconst], writes=[bP[pi]])

        def back(i):
            p, dh, bi = steps[i]
            g, kc0, vb, nk, q0, kind = sblocks[p][bi]
            pi, nq = info[i]
            first, last = (bi == 0), (bi == len(sblocks[p]) - 1)

            def pvm(e):
                rr = None
                for m in range(2):
                    rr = e.matmul(OO[:, m * T + q0:(m + 1) * T],
                                  lhsT=V[0:nk, vb * 512 + dh * 128: vb * 512 + (dh + 1) * 128],
                                  rhs=Pm[pi][0:nk, m * T: m * T + nq], start=first, stop=last)
                    rr = e.matmul(LL[:, m * T + q0:(m + 1) * T], lhsT=onesb[0:nk, :],
                                  rhs=Pm[pi][0:nk, m * T: m * T + nq], start=first, stop=last)
                return rr
            P.op("pe", pvm, reads=[bV[g], bP[pi], bconst], writes=[bO, bL])

        def fin_stages(dh):
            def stA():
                P.op("dve", lambda e: e.tensor_copy(out=r2[:], in_=LL[:, :]), reads=[bL], writes=[br2])
                P.op("act", lambda e: e.activation(out=o12[:], in_=OO[:, :], func=AF.Copy), reads=[bO],
                     writes=[bfin])
                P.op("dve", lambda e: e.reciprocal(out=r2[:], in_=r2[:]), writes=[br2])
                P.op("dve", lambda e: e.tensor_tensor(out=o12[:], in0=o12[:], in1=r2[:], op=ALU.mult),
                     reads=[br2], writes=[bfin])
                P.op("dve", lambda e: e.scalar_tensor_tensor(out=o1[:], in0=o12[:, T:2 * T], scalar=neglam,
                                                             in1=o12[:, 0:T], op0=ALU.mult, op1=ALU.add),
                     reads=[blam], writes=[bfin])

            def stB():
                P.op("act", lambda e: e.activation(out=rr_[:], in_=o1[:], func=AF.Square), reads=[bfin],
                     writes=[bfin])

            ssi = {}

            def stC():
                si = cnt["s"] % 2
                cnt["s"] += 1
                ssi["si"] = si
                P.op("pe", lambda e: e.matmul(S[si][:, 0:T], lhsT=onesf[:], rhs=rr_[:], start=True, stop=True),
                     reads=[bfin, bconst], writes=[bS[si]])

            def stD():
                si = ssi["si"]
                P.op("dve", lambda e: e.tensor_scalar(out=rr_[:], in0=S[si][:, 0:T], scalar1=1.0 / 128,
                                                      scalar2=EPS, op0=ALU.mult, op1=ALU.add),
                     reads=[bS[si]], writes=[bfin])

            def stE():
                P.op("act", lambda e: e.activation(out=rr_[:], in_=rr_[:], func=AF.Sqrt), writes=[bfin])

            def stF():
                P.op("dve", lambda e: e.reciprocal(out=rr_[:], in_=rr_[:]), writes=[bfin])
                P.op("dve", lambda e: e.tensor_tensor(out=o1[:], in0=o1[:], in1=rr_[:], op=ALU.mult),
                     writes=[bfin])

            def stG():
                P.op("act", lambda e: e.activation(out=aT[:, dh * T:(dh + 1) * T], in_=o1[:], func=AF.Copy,
                                                   scale=sw[:, 1:2]), reads=[bfin, blam], writes=[baT])
            return [stA, stB, stC, stD, stE, stF, stG]

        def out_proj(p):
            for b in range(4):
                k = b % 2
                si = cnt["s"] % 2
                cnt["s"] += 1

                def omm(e, b=b, si=si):
                    rr = None
                    for hf in range(2):
                        for c in range(8):
                            src = aT if c < 4 else cT
                            cc = c % 4
                            rr = e.matmul(S[si][:, hf * 512:(hf + 1) * 512],
                                          lhsT=src[:, cc * T + b * 128: cc * T + (b + 1) * 128],
                                          rhs=Wo[:, c * D + hf * 512: c * D + (hf + 1) * 512],
                                          start=(c == 0), stop=(c == 7))
                    return rr
                P.op("pe", omm, reads=[baT, bcT, bWo], writes=[bS[si]])
                P.op("dve", lambda e, si=si, k=k: e.tensor_tensor(out=hb[k][:], in0=S[si][:, :], in1=hb[k][:],
                                                                  op=ALU.add),
                     reads=[bS[si]], writes=[bhb[k]])
                r0 = p * T + b * 128
                P.op("sp", lambda e, k=k, r0=r0: [e.dma_start(out=h2_s[r0:r0 + 128, :], in_=hb[k][:])],
                     reads=[bhb[k]], dsem=s_hb[k])
                if b + 2 < 4:
                    load_hb(p, b + 2)
            if p + 1 < NPAIR:
                load_c(p + 1)
                load_hb(p + 1, 0)
                load_hb(p + 1, 1)

        load_c(0)
        load_hb(0, 0)
        load_hb(0, 1)
        pending = []
        nsteps = len(steps)
        front(0)
        front(1)
        for i in range(nsteps):
            p, dh, bi = steps[i]
            if dh == 0 and bi == 0 and p + 1 < NPAIR:
                load_q(p + 1)
            if i + 2 < nsteps:
                front(i + 2)
            back(i)
            if bi == len(sblocks[p]) - 1:
                while pending:
                    pending.pop(0)[1]()
                st_ = fin_stages(dh)
                st_[0]()
                for k_, f_ in enumerate(st_[1:]):
                    pending.append((i + 9 + k_, f_))
                if dh == 3:
                    pending.append((i + 15, lambda p=p: out_proj(p)))
            while pending and pending[0][0] <= i:
                pending.pop(0)[1]()
        for (_, f_) in pending:
            f_()
        P.drain()
        P.emit(nc)


def build_nc(NPAIR, FF):
    NT = 1 + NPAIR
    T = 512
    nc = bass.Bass("TRN2", target_bir_lowering=False)
    di = lambda n, s: nc.dram_tensor(n, s, F32, kind="ExternalInput").ap()
    xs = di("xs", [NT * T, D])
    cos_t = di("cos_t", [128, NT * T])
    sin_t = di("sin_t", [128, NT * T])
    hmask = di("hmask", [128, 256])
    wpos = di("wpos", [2, 128, 896])
    ident = di("ident", [128, 128])
    g1 = di("ffn1_norm", [D])
    wg1 = di("ffn1_w_gate", [D, FF])
    wu1 = di("ffn1_w_up", [D, FF])
    wd1 = di("ffn1_w_down", [FF, D])
    gm = di("mix_norm", [D])
    win = di("w_in", [D, 2560])
    lq1 = di("lambda_q1", [64])
    lk1 = di("lambda_k1", [64])
    lq2 = di("lambda_q2", [64])
    lk2 = di("lambda_k2", [64])
    subw = di("subln_w", [128])
    convw = di("conv_w", [31, 512])
    convb = di("conv_b", [512])
    lng = di("conv_ln_g", [512])
    lnb = di("conv_ln_b", [512])
    wout = di("w_out", [D, D])
    g2 = di("ffn2_norm", [D])
    wg2 = di("ffn2_w_gate", [D, FF])
    wu2 = di("ffn2_w_up", [D, FF])
    wd2 = di("ffn2_w_down", [FF, D])
    gf = di("final_norm", [D])
    out = nc.dram_tensor("out", [NPAIR * T, D], F32, kind="ExternalOutput").ap()
    h1_s = nc.dram_tensor("h1_s", [NT * T, D], F32, kind="Internal").ap()
    h2_s = nc.dram_tensor("h2_s", [NPAIR * T, D], F32, kind="Internal").ap()
    KTm_s = nc.dram_tensor("KTm_s", [128, 4 * 128], BF16, kind="Internal").ap()
    Vm_s = nc.dram_tensor("Vm_s", [128, 512], BF16, kind="Internal").ap()
    Q_s = nc.dram_tensor("Q_s", [NPAIR, 128, 4 * T], BF16, kind="Internal").ap()
    c_s = nc.dram_tensor("c_s", [NPAIR, 128, 4 * T], BF16, kind="Internal").ap()
    send = [nc.dram_tensor("send%d" % i, [256, 2048], BF16).ap() for i in range(NPAIR)]
    recv = [nc.dram_tensor("recv%d" % i, [512, 2048], BF16).ap() for i in range(NPAIR)]

    sc = lambda n, shp: nc.dram_tensor(n, shp, BF16, kind="Internal").ap()
    wg2b, wu2b, wd2b = sc("wg2b", [D, FF]), sc("wu2b", [D, FF]), sc("wd2b", [FF, D])
    winb, woutb = sc("winb", [D, 2560]), sc("woutb", [D, D])
    pc = [(win, winb), (wout, woutb), (wg2, wg2b), (wu2, wu2b), (wd2, wd2b)]
    ffn_phase(nc, "a1", xs, h1_s, NT * T, wg1, wu1, wd1, g1, FF, ident, precast=pc)
    proj_phase(nc, "a2", NPAIR, h1_s, winb, gm, cos_t, sin_t, hmask, convw, convb, lng, lnb, ident,
               KTm_s, Vm_s, Q_s, c_s, send, recv)
    attn_phase(nc, "b", NPAIR, KTm_s, Vm_s, recv, Q_s, c_s, h1_s, h2_s, woutb, lq1, lk1, lq2, lk2, subw, wpos)
    ffn_phase(nc, "c", h2_s, out, NPAIR * T, wg2b, wu2b, wd2b, g2, FF, ident, final_g=gf, wq="sp")
    return nc


def host_layout(x, meta_tokens, NPAIR):
    B, SEQ, _ = x.shape
    T = 512
    NT = 1 + NPAIR
    assert SEQ == 2 * NPAIR * T
    inv_freq = (np.float32(10000.0) ** (-np.arange(0, 64, 2, dtype=np.float32) / np.float32(64))).astype(np.float32)
    pidx = np.arange(128) % 32
    sgn = np.where((np.arange(128) % 64) < 32, -1.0, 1.0).astype(np.float32)
    tri = (np.arange(128)[:, None] <= np.arange(128)[None, :]).astype(np.float32)
    wdiag = np.concatenate([np.zeros((128, 384), np.float32), tri, np.ones((128, 384), np.float32)], axis=1)
    ident = np.eye(128, dtype=np.float32)
    per_core = []
    for core in range(2 * B):
        b, role = core // 2, core % 2
        xs = np.zeros((NT * T, D), np.float32)
        pos = np.zeros((NT * T,), np.float32)
        hmask = np.ones((128, 256), np.float32)
        xs[0:N_META] = meta_tokens
        pos[0:N_META] = np.arange(N_META)
        for p in range(NPAIR):
            oc = 2 * p + role
            r0 = 128 + p * 32
            if oc == 0:
                xs[r0 + 16: r0 + 32] = meta_tokens
                hmask[:, p * 32: p * 32 + 16] = 0.0
            else:
                xs[r0: r0 + 32] = x[b, oc * T - 32: oc * T]
            tile = 1 + p
            xs[tile * T:(tile + 1) * T] = x[b, oc * T:(oc + 1) * T]
            pos[tile * T:(tile + 1) * T] = N_META + oc * T + np.arange(T)
        ang = pos[None, :].astype(np.float32) * inv_freq[pidx][:, None]
        cos_t = np.cos(ang).astype(np.float32)
        sin_t = (np.sin(ang) * sgn[:, None]).astype(np.float32)
        if role == 0:
            wpos = np.stack([wdiag, np.zeros_like(wdiag)])
        else:
            wpos = np.stack([np.ones_like(wdiag), wdiag])
        per_core.append(dict(xs=xs, cos_t=cos_t, sin_t=sin_t, hmask=hmask, wpos=np.ascontiguousarray(wpos),
                             ident=ident))
    return per_core


WNAMES = ["ffn1_norm", "ffn1_w_gate", "ffn1_w_up", "ffn1_w_down", "mix_norm", "w_in", "lambda_q1", "lambda_k1",
          "lambda_q2", "lambda_k2", "subln_w", "conv_w", "conv_b", "conv_ln_g", "conv_ln_b", "w_out", "ffn2_norm",
          "ffn2_w_gate", "ffn2_w_up", "ffn2_w_down"]


def kernel(**inputs):
    x = np.asarray(inputs["x"], np.float32)
    B, SEQ, _ = x.shape
    T = 512
    NPAIR = SEQ // (2 * T)
    FF = inputs["ffn1_w_gate"].shape[-1]
    meta = np.asarray(inputs["meta_tokens"], np.float32)
    per_core = host_layout(x, meta, NPAIR)
    shared = {n: np.ascontiguousarray(np.asarray(inputs[n], np.float32)[0]) for n in WNAMES}
    shared["final_norm"] = np.ascontiguousarray(np.asarray(inputs["final_norm"], np.float32))
    in_maps = []
    for c in range(2 * B):
        m = dict(shared)
        m.update(per_core[c])
        in_maps.append(m)
    global RGROUPS
    RGROUPS = [[2 * i, 2 * i + 1] for i in range(B)]
    nc = build_nc(NPAIR, FF)
    res = run_bass_kernel_spmd(nc, in_maps, core_ids=list(range(2 * B)))
    out = np.zeros((B, SEQ, D), np.float32)
    for c in range(2 * B):
        b, role = c // 2, c % 2
        o = np.asarray(res.results[c]["out"])
        for p in range(NPAIR):
            oc = 2 * p + role
            out[b, oc * T:(oc + 1) * T] = o[p * T:(p + 1) * T]
    return out
```
